# Optimizing a Trainium2 kernel written in Bass

```python
import math
import jax, jax.numpy as jnp
from jax import lax
import numpy as np

D_MODEL = 1024
BATCH = 4
SEQ = 8192
DEPTH = 4

PLE_DIM = 256
BRANCH_W = 512
N_BRANCH = 3
A_HEADS = 8
A_NOPE = 64
A_ROPE = 32
A_VDIM = 64
A_Q_LORA = 256
A_KV_LORA = 128
A_SCALE = (A_NOPE + A_ROPE) ** -0.5
ROPE_THETA = 10000.0
Q_BLOCK = 128
B_HEADS = 4
B_DK = 128
B_DV = 128
B_CONV = 4
B_CHUNK = 64
C_HEADS = 4
C_DK = 128
C_DV = 128
C_CHUNK = 32

NORM_EPS = 1e-6
MASK_VALUE = -1e30
DEEPNORM_ALPHA = (2.0 * DEPTH) ** 0.25
DEEPNORM_BETA = (8.0 * DEPTH) ** -0.25

IN_SPLITS = (
    A_Q_LORA,
    A_KV_LORA,
    A_ROPE,
    BRANCH_W,
    3 * B_HEADS * B_DK,
    B_HEADS,
    B_HEADS,
    BRANCH_W,
    C_HEADS * C_DK,
    C_HEADS * C_DK,
    C_HEADS * C_DV,
    BRANCH_W,
    N_BRANCH * D_MODEL,
)
IN_WIDTH = int(sum(IN_SPLITS))
IN_OFFSETS = tuple(int(o) for o in np.cumsum(IN_SPLITS)[:-1])

kernel_name = 'hybrid_mla_gdn_hgrn2_block'


def rms_norm(x, g):
    xf = x.astype(jnp.float32)
    y = xf * lax.rsqrt(jnp.mean(xf * xf, axis=-1, keepdims=True) + NORM_EPS)
    return y.astype(x.dtype) * g


def layer_norm(x, g, b):
    xf = x.astype(jnp.float32)
    mu = jnp.mean(xf, axis=-1, keepdims=True)
    var = jnp.mean(jnp.square(xf - mu), axis=-1, keepdims=True)
    return ((xf - mu) * lax.rsqrt(var + NORM_EPS)).astype(x.dtype) * g + b


def l2_norm(x):
    return x * lax.rsqrt(jnp.sum(x * x, axis=-1, keepdims=True) + NORM_EPS)


def masked_exp(diff, mask):
    return jnp.where(mask, jnp.exp(jnp.where(mask, diff, 0.0)), 0.0)


def rope_tables(positions):
    inv = ROPE_THETA ** (-jnp.arange(0, A_ROPE, 2, dtype=jnp.float32) / A_ROPE)
    ang = positions.astype(jnp.float32)[..., None] * inv
    return jnp.cos(ang), jnp.sin(ang)


def apply_rope(x, cos, sin):
    xf = x.astype(jnp.float32)
    half = xf.shape[-1] // 2
    x1, x2 = xf[..., :half], xf[..., half:]
    return jnp.concatenate([x1 * cos - x2 * sin, x2 * cos + x1 * sin], axis=-1).astype(x.dtype)


def causal_depthwise_conv(x, w):
    width = w.shape[0]
    return lax.conv_general_dilated(
        x, w.astype(x.dtype), window_strides=(1,), padding=[(width - 1, 0)],
        dimension_numbers=('NWC', 'WIO', 'NWC'), feature_group_count=x.shape[-1])


def to_chunks(t, c):
    b, s, h = t.shape[:3]
    return jnp.moveaxis(t.reshape(b, s // c, c, h, *t.shape[3:]), 3, 1)


def from_chunks(o):
    n, b, h, c, d = o.shape
    return jnp.transpose(o, (1, 0, 3, 2, 4)).reshape(b, n * c, h, d)


def causal_block_attention(q_nope, q_rope, k_nope, k_rope, v):
    b, s, h, _ = q_nope.shape
    nblk = s // Q_BLOCK
    key_idx = jnp.arange(s)

    def blocks(t):
        return jnp.moveaxis(t.reshape(b, nblk, Q_BLOCK, *t.shape[2:]), 1, 0)

    def one_block(args):
        qn, qr, start = args
        scores = (jnp.einsum('bqhd,bkhd->bhqk', qn, k_nope)
                  + jnp.einsum('bqhd,bkd->bhqk', qr, k_rope)).astype(jnp.float32) * A_SCALE
        q_idx = start + jnp.arange(Q_BLOCK)
        mask = key_idx[None, :] <= q_idx[:, None]
        probs = jax.nn.softmax(jnp.where(mask, scores, MASK_VALUE), axis=-1).astype(v.dtype)
        return jnp.einsum('bhqk,bkhd->bqhd', probs, v)

    starts = jnp.arange(nblk, dtype=jnp.int32) * Q_BLOCK
    out = lax.map(one_block, (blocks(q_nope), blocks(q_rope), starts))
    return jnp.moveaxis(out, 0, 1).reshape(b, s, h, v.shape[-1])


def chunk_gated_delta_rule(q, k, v, g, beta):
    b, s, h, dk = q.shape
    dv = v.shape[-1]
    c = B_CHUNK
    q = to_chunks(q, c) * dk ** -0.5
    k = to_chunks(k, c)
    v = to_chunks(v, c)
    beta = to_chunks(beta, c)
    g = jnp.cumsum(to_chunks(g, c), axis=-1)
    tril = jnp.tril(jnp.ones((c, c), bool))
    strict = jnp.tril(jnp.ones((c, c), bool), -1)
    decay = masked_exp(g[..., :, None] - g[..., None, :], tril)
    k_beta = k * beta[..., None]
    low = jnp.where(strict, jnp.einsum('bhnid,bhnjd->bhnij', k_beta, k) * decay, 0.0)
    a_mat = low + jnp.eye(c, dtype=low.dtype)
    rhs = jnp.concatenate([v * beta[..., None], k_beta * jnp.exp(g)[..., None]], axis=-1)
    sol = lax.linalg.triangular_solve(a_mat, rhs, left_side=True, lower=True, unit_diagonal=True)
    u, w = sol[..., :dv], sol[..., dv:]
    intra = jnp.einsum('bhnid,bhnjd->bhnij', q, k) * decay
    q_dec = q * jnp.exp(g)[..., None]
    g_last = g[..., -1]
    k_dec = k * jnp.exp(g_last[..., None] - g)[..., None]

    def step(state, xs):
        u_c, w_c, a_c, qd_c, kd_c, gl_c = xs
        v_new = u_c - jnp.einsum('bhid,bhde->bhie', w_c, state)
        out = (jnp.einsum('bhid,bhde->bhie', qd_c, state)
               + jnp.einsum('bhij,bhje->bhie', a_c, v_new))
        state = state * jnp.exp(gl_c)[..., None, None] + jnp.einsum('bhid,bhie->bhde', kd_c, v_new)
        return state, out

    xs = tuple(jnp.moveaxis(t, 2, 0) for t in (u, w, intra, q_dec, k_dec, g_last))
    state0 = jnp.zeros((b, h, dk, dv), jnp.float32)
    _, out = lax.scan(step, state0, xs)
    return from_chunks(out)


def chunk_hgrn2(q, k, v, log_f):
    b, s, h, dk = q.shape
    dv = v.shape[-1]
    c = C_CHUNK
    q, k, v, log_f = (to_chunks(t, c) for t in (q, k, v, log_f))
    cum = jnp.cumsum(log_f, axis=-2)
    cum_last = cum[..., -1, :]
    q_dec = q * jnp.exp(cum)
    k_dec = k * jnp.exp(cum_last[..., None, :] - cum)
    causal = jnp.tril(jnp.ones((c, c), bool))[:, :, None]

    def step(state, xs):
        q_c, k_c, cum_c, qd_c, kd_c, v_c, cl_c = xs
        rel = masked_exp(cum_c[..., :, None, :] - cum_c[..., None, :, :], causal)
        scores = jnp.einsum('bhid,bhjd,bhijd->bhij', q_c, k_c, rel)
        out = (jnp.einsum('bhid,bhde->bhie', qd_c, state)
               + jnp.einsum('bhij,bhje->bhie', scores, v_c))
        state = jnp.exp(cl_c)[..., None] * state + jnp.einsum('bhjd,bhje->bhde', kd_c, v_c)
        return state, out

    xs = tuple(jnp.moveaxis(t, 2, 0) for t in (q, k, cum, q_dec, k_dec, v, cum_last))
    state0 = jnp.zeros((b, h, dk, dv), jnp.float32)
    _, out = lax.scan(step, state0, xs)
    return from_chunks(out)


def mla_branch(cq, ckv, kr, za, q_norm, w_uq, kv_norm, w_ukv, cos, sin):
    b, s, _ = cq.shape
    qa = (rms_norm(cq, q_norm) @ w_uq).reshape(b, s, A_HEADS, A_NOPE + A_ROPE)
    q_nope = qa[..., :A_NOPE]
    q_rope = apply_rope(qa[..., A_NOPE:], cos[:, :, None, :], sin[:, :, None, :])
    kv = (rms_norm(ckv, kv_norm) @ w_ukv).reshape(b, s, A_HEADS, A_NOPE + A_VDIM)
    k_nope, v = kv[..., :A_NOPE], kv[..., A_NOPE:]
    k_rope = apply_rope(kr, cos, sin)
    o = causal_block_attention(q_nope, q_rope, k_nope, k_rope, v)
    return o.reshape(b, s, BRANCH_W) * jax.nn.silu(za)


def gdn_branch(qkv, ba, bb, zb, conv_w, a_log, dt_bias, norm_g):
    b, s, _ = qkv.shape
    qkv = jax.nn.silu(causal_depthwise_conv(qkv, conv_w)).astype(jnp.float32)
    q, k, v = (t.reshape(b, s, B_HEADS, -1) for t in jnp.split(qkv, 3, axis=-1))
    q, k = l2_norm(q), l2_norm(k)
    g = -jnp.exp(a_log.astype(jnp.float32)) * jax.nn.softplus(
        ba.astype(jnp.float32) + dt_bias.astype(jnp.float32))
    beta = jax.nn.sigmoid(bb.astype(jnp.float32))
    o = chunk_gated_delta_rule(q, k, v, g, beta)
    o = rms_norm(o, norm_g).astype(zb.dtype)
    return o.reshape(b, s, BRANCH_W) * jax.nn.silu(zb)


def hgrn2_branch(cq, cf, ci, zc, lower_bound, norm_g):
    b, s, _ = cq.shape
    q = (jax.nn.silu(cq.astype(jnp.float32)) * C_DK ** -0.5).reshape(b, s, C_HEADS, C_DK)
    z = cf.astype(jnp.float32).reshape(b, s, C_HEADS, C_DK)
    lb = lower_bound.astype(jnp.float32).reshape(C_HEADS, C_DK)
    sig = jax.nn.sigmoid(z)
    f = lb + (1.0 - lb) * sig
    log_f = jnp.log(jnp.maximum(f, 1e-30))
    k = (1.0 - lb) * (1.0 - sig)
    v = ci.astype(jnp.float32).reshape(b, s, C_HEADS, C_DV)
    o = chunk_hgrn2(q, k, v, log_f)
    o = rms_norm(o, norm_g).astype(zc.dtype)
    return o.reshape(b, s, BRANCH_W) * jax.nn.silu(zc)


def hgrn_lower_bounds(logits):
    pr = jax.nn.softmax(logits.astype(jnp.float32), axis=0)
    return jnp.clip(jnp.cumsum(pr, axis=0) - pr[0:1], 0.0, 1.0 - 1e-6)


def setup_inputs(seed: int = 0) -> dict:
    key = jax.random.key(seed)
    ks = jax.random.split(key, 24)
    f32 = jnp.float32
    L, D = DEPTH, D_MODEL

    def normal(k, shape, scale):
        return jax.random.normal(k, shape, f32) * scale

    def gain(k, shape):
        return 1.0 + 0.01 * jax.random.normal(k, shape, f32)

    x = jax.random.normal(ks[0], (BATCH, SEQ, D), f32)
    p = jax.random.normal(ks[1], (DEPTH, BATCH, SEQ, PLE_DIM), f32)
    offsets = jax.random.randint(ks[2], (BATCH, 1), 0, 4096, dtype=jnp.int32)
    positions = offsets + jnp.arange(SEQ, dtype=jnp.int32)[None, :]
    w_in = normal(ks[3], (L, D, IN_WIDTH), D ** -0.5)
    a_q_norm = gain(ks[4], (L, A_Q_LORA))
    a_w_uq = normal(ks[5], (L, A_Q_LORA, A_HEADS * (A_NOPE + A_ROPE)), A_Q_LORA ** -0.5)
    a_kv_norm = gain(ks[6], (L, A_KV_LORA))
    a_w_ukv = normal(ks[7], (L, A_KV_LORA, A_HEADS * (A_NOPE + A_VDIM)), A_KV_LORA ** -0.5)
    b_conv = normal(ks[8], (L, B_CONV, 1, 3 * B_HEADS * B_DK), B_CONV ** -0.5)
    b_a_log = jnp.log(jax.random.uniform(ks[9], (L, B_HEADS), f32, 1.0, 16.0))
    dt = jnp.exp(jax.random.uniform(ks[10], (L, B_HEADS), f32, math.log(1e-3), math.log(1e-1)))
    b_dt_bias = dt + jnp.log(-jnp.expm1(-dt))
    b_norm = gain(ks[11], (L, B_DV))
    c_lb_logits = normal(ks[12], (L, C_HEADS * C_DK), 1.0)
    c_norm = gain(ks[13], (L, C_DV))
    w_branch = normal(ks[14], (L, N_BRANCH, BRANCH_W, D), BRANCH_W ** -0.5 * DEEPNORM_BETA)
    w_out = normal(ks[15], (L, D, D), D ** -0.5 * DEEPNORM_BETA)
    ple_proj = normal(ks[16], (L, PLE_DIM, D), PLE_DIM ** -0.5)
    ple_gate = normal(ks[17], (L, D, D), D ** -0.5)
    ln_g = gain(ks[18], (L, D))
    ln_b = normal(ks[19], (L, D), 0.01)
    return {'x': x, 'p': p, 'positions': positions, 'w_in': w_in,
            'a_q_norm': a_q_norm, 'a_w_uq': a_w_uq, 'a_kv_norm': a_kv_norm, 'a_w_ukv': a_w_ukv,
            'b_conv': b_conv, 'b_a_log': b_a_log, 'b_dt_bias': b_dt_bias, 'b_norm': b_norm,
            'c_lb_logits': c_lb_logits, 'c_norm': c_norm, 'w_branch': w_branch, 'w_out': w_out,
            'ple_proj': ple_proj, 'ple_gate': ple_gate, 'ln_g': ln_g, 'ln_b': ln_b}


def reference(x, p, positions, w_in, a_q_norm, a_w_uq, a_kv_norm, a_w_ukv, b_conv, b_a_log,
              b_dt_bias, b_norm, c_lb_logits, c_norm, w_branch, w_out, ple_proj, ple_gate,
              ln_g, ln_b):
    b, s, _ = x.shape
    cos, sin = rope_tables(positions)
    lower_bounds = hgrn_lower_bounds(c_lb_logits)
    for i in range(DEPTH):
        h = x @ w_in[i]
        (cq, ckv, kr, za, qkv, ba, bb, zb, hq, hf, hi, zc, gate_logits) = jnp.split(
            h, IN_OFFSETS, axis=-1)
        y_a = mla_branch(cq, ckv, kr, za, a_q_norm[i], a_w_uq[i], a_kv_norm[i], a_w_ukv[i],
                         cos, sin)
        y_b = gdn_branch(qkv, ba, bb, zb, b_conv[i], b_a_log[i], b_dt_bias[i], b_norm[i])
        y_c = hgrn2_branch(hq, hf, hi, zc, lower_bounds[i], c_norm[i])
        branches = jnp.stack([y_a, y_b, y_c], axis=2)
        proj = jnp.einsum('bsnw,nwd->bsnd', branches, w_branch[i])
        gates = jax.nn.sigmoid(gate_logits).reshape(b, s, N_BRANCH, D_MODEL)
        merged = jnp.einsum('bsnd,bsnd->bsd', gates, proj)
        r = DEEPNORM_ALPHA * x + merged @ w_out[i]
        r = r + jax.nn.sigmoid(r @ ple_gate[i]) * (p[i] @ ple_proj[i])
        x = layer_norm(r, ln_g[i], ln_b[i])
    return x
```

```python
import math
from contextlib import ExitStack
import numpy as np
import concourse.bass as bass
import concourse.mybir as mybir
from concourse.bass_utils import run_bass_kernel_spmd

F32 = mybir.dt.float32
BF16 = mybir.dt.bfloat16
I32 = mybir.dt.int32
AF = mybir.ActivationFunctionType
ALU = mybir.AluOpType

D = 1024
NL = 4
PLE = 256
INW = 8104
A_SCALE = 96 ** -0.5
EPS = 1e-6
ALPHA = (2.0 * 4) ** 0.25
O_CQ, O_CKV, O_KR, O_ZA, O_QKV, O_BA, O_BB, O_ZB, O_HQ, O_HF, O_HI, O_ZC, O_G = (
    0, 256, 384, 416, 928, 2464, 2468, 2472, 2984, 3496, 4008, 4520, 5032)
NP1 = 5032
TT = 512


class Buf:
    __slots__ = ("last_w", "readers")

    def __init__(self):
        self.last_w = None
        self.readers = []


class Op:
    __slots__ = ("eng", "fn", "waits", "signal", "dsem", "count", "small", "cost", "lat", "tab")

    def __init__(self, eng, fn, dsem=None, small=False, cost=300.0, lat=0.0, tab=None):
        self.eng = eng
        self.fn = fn
        self.waits = []
        self.signal = False
        self.dsem = dsem
        self.count = None
        self.small = small
        self.cost = cost
        self.lat = lat
        self.tab = tab


ENGS = ("pe", "act", "dve", "pool", "sp")
TABS = {AF.Exp: "A", AF.Ln: "A", AF.Tanh: "A", AF.Sigmoid: "B", AF.Silu: "C", AF.Sin: "D"}


class KB:
    def __init__(self, nc):
        self.nc = nc
        self.sems = {}
        self.counts = {}
        self.seen = {e: {} for e in ENGS}
        self.ops = []
        self.ndsem = 0
        self.ninst = 0
        self.reorder = True
        self.window = 0
        self.sim_total = 0.0

    def new_dsem(self):
        self.ndsem += 1
        return ("d", self.ndsem - 1)

    def begin(self):
        self.ops = []
        self.ndsem = 0
        self.ld_sem = {}
        self.st_sem = {}
        self.nd2 = 0
        self._keep = []

    def op(self, eng, fn, reads=(), writes=(), dsem=None, small=False, cost=300.0, lat=0.0, tab=None):
        idx = len(self.ops)
        o = Op(eng, fn, dsem, small, cost, lat, tab)
        deps = set()
        for b in reads:
            if b.last_w is not None:
                deps.add(b.last_w)
        for b in writes:
            if b.last_w is not None:
                deps.add(b.last_w)
            deps.update(b.readers)
        for b in reads:
            b.readers.append(idx)
        for b in writes:
            b.last_w = idx
            b.readers = []
        deps.discard(idx)
        o.waits = sorted(deps)
        self.ops.append(o)
        return idx

    def dma(self, out, in_, reads, writes, dsem=None, q="sp"):
        if len(reads) == 0:
            kbuf, tab = writes[0], self.ld_sem
        else:
            kbuf, tab = reads[0], self.st_sem
        if id(kbuf) not in tab:
            tab[id(kbuf)] = ("d", self.nd2)
            self.nd2 += 1
            self._keep.append(kbuf)
        nbytes = 1
        for d in out.shape:
            nbytes *= d
        return self.op(q, lambda e: e.dma_start(out=out, in_=in_), reads, writes, dsem=tab[id(kbuf)], cost=60.0, lat=2000.0 + nbytes * 4 / 100.0)

    def _schedule(self, ops):
        import heapq
        n = len(ops)
        succ = [[] for _ in range(n)]
        ndep = [0] * n
        for i, o in enumerate(ops):
            ndep[i] = len(o.waits)
            for d in o.waits:
                succ[d].append(i)
        ready_t = [0.0] * n
        fin = [0.0] * n
        pend = {e: [] for e in ENGS}
        avail = {e: [] for e in ENGS}
        clock = {e: 0.0 for e in ENGS}
        for i, o in enumerate(ops):
            if ndep[i] == 0:
                heapq.heappush(pend[o.eng], (0.0, i))
        order = []
        done = 0
        cur_tab = None
        act_av = {}
        SWITCH = 1400.0
        while done < n:
            best = None
            for e in ENGS:
                pe_, av = pend[e], avail[e]
                while pe_ and pe_[0][0] <= clock[e]:
                    i_ = heapq.heappop(pe_)[1]
                    if e == "act":
                        heapq.heappush(act_av.setdefault(ops[i_].tab, []), i_)
                    else:
                        heapq.heappush(av, i_)
                if e == "act":
                    c1 = [h_[0] for t_, h_ in act_av.items() if h_ and (t_ is None or t_ == cur_tab)]
                    c2 = [h_[0] for t_, h_ in act_av.items() if h_]
                    if c1:
                        cand = (clock[e], min(c1), e, True)
                    elif c2:
                        cand = (clock[e] + SWITCH, min(c2), e, True)
                    elif pe_:
                        cand = (pe_[0][0], pe_[0][1], e, False)
                    else:
                        continue
                elif av:
                    cand = (clock[e], av[0], e, True)
                elif pe_:
                    cand = (pe_[0][0], pe_[0][1], e, False)
                else:
                    continue
                if best is None or cand[:2] < best[:2]:
                    best = cand
            st, i, e, from_av = best
            if e == "act":
                if from_av:
                    heapq.heappop(act_av[ops[i].tab])
                else:
                    heapq.heappop(pend[e])
                    if ops[i].tab is not None and ops[i].tab != cur_tab:
                        st += SWITCH
                if ops[i].tab is not None:
                    cur_tab = ops[i].tab
            elif from_av:
                heapq.heappop(avail[e])
            else:
                heapq.heappop(pend[e])
            o = ops[i]
            clock[e] = st + o.cost
            fin[i] = st + o.cost + o.lat
            order.append(i)
            done += 1
            for j in succ[i]:
                ndep[j] -= 1
                lat_x = 0.0 if ops[j].eng == e and o.dsem is None else 180.0
                if fin[i] + lat_x > ready_t[j]:
                    ready_t[j] = fin[i] + lat_x
                if ndep[j] == 0:
                    heapq.heappush(pend[ops[j].eng], (ready_t[j], j))
        self.sim_time = max(fin) if fin else 0.0
        pos = {old: new for new, old in enumerate(order)}
        out = []
        for old in order:
            o = ops[old]
            o.waits = sorted(pos[d] for d in o.waits)
            out.append(o)
        return out

    def _sem(self, k):
        if k not in self.sems:
            nm = "s_" + (k if isinstance(k, str) else "d%d" % k[1])
            self.sems[k] = self.nc.alloc_semaphore(nm)
        return self.sems[k]

    def end(self):
        nc = self.nc
        ops = self.ops
        if self.reorder and len(ops) > 1:
            ops = self._schedule(ops)

        def key(o):
            return o.dsem if o.dsem is not None else o.eng

        for o in ops:
            kept = []
            for d in o.waits:
                p = ops[d]
                if p.dsem is None and p.eng == o.eng and not (p.small or o.small):
                    continue
                p.signal = True
                kept.append(d)
            o.waits = kept
        dkeys = set()
        for o in ops:
            if o.dsem is not None:
                o.signal = True
                dkeys.add(o.dsem)
            if o.signal:
                k = key(o)
                self.counts[k] = self.counts.get(k, 0) + (16 if o.dsem is not None else 1)
                o.count = self.counts[k]
        plan = {e: [] for e in ENGS}
        for o in ops:
            need = {}
            for d in o.waits:
                p = ops[d]
                k = key(p)
                if p.count > self.seen[o.eng].get(k, 0):
                    need[k] = max(need.get(k, 0), p.count)
            for k, v in need.items():
                self.seen[o.eng][k] = v
            plan[o.eng].append((o, sorted(need.items(), key=str)))
        final = [(k, self.counts[k]) for k in sorted(dkeys, key=str)]
        for k, v in final:
            self.seen["sp"][k] = max(self.seen["sp"].get(k, 0), v)
        for o in ops:
            if o.signal:
                self._sem(key(o))
        sems = self.sems
        self.ninst += len(ops)

        with nc.Block() as block:
            def run(engname):
                def body(e):
                    for o, waits in plan[engname]:
                        for k, v in waits:
                            e.wait_ge(sems[k], v)
                        ins = o.fn(e)
                        if o.signal:
                            ins.then_inc(sems[key(o)], 16 if o.dsem is not None else 1)
                    if engname == "sp":
                        for k, v in final:
                            e.wait_ge(sems[k], v)
                return body

            block.tensor(run("pe"))
            block.scalar(run("act"))
            block.vector(run("dve"))
            block.gpsimd(run("pool"))
            block.sync(run("sp"))
        nc.all_engine_barrier()
        self.ops = []


class Tile:
    def __init__(self, t, nb=1):
        self.t = t
        self.b = Buf()
        self.bs = [Buf() for _ in range(nb)] if nb > 1 else [self.b]


C_ID, C_UTI, C_UTS, C_CM, C_M32, C_M128, C_C32, C_SELB, C_INV, C_ONE, C_END = (
    0, 128, 256, 384, 384 + 2048, 384 + 2560, 384 + 3072, 384 + 3104, 384 + 3104 + 1024,
    384 + 3104 + 1025, 384 + 3104 + 1025 + 128)


def make_consts():
    c = np.zeros((128, C_END), np.float32)
    i = np.arange(128)
    c[:, C_ID:C_ID + 128] = np.eye(128)
    c[:, C_UTI:C_UTI + 128] = (i[:, None] <= i[None, :])
    c[:, C_UTS:C_UTS + 128] = (i[:, None] < i[None, :])
    q = np.arange(512)
    for m in range(4):
        c[:, C_CM + 512 * m:C_CM + 512 * (m + 1)] = (q[None, :] - i[:, None] - 128 * m >= 0)
    c[:, C_M32:C_M32 + 512] = (q % 32 != 0)[None, :]
    c[:, C_M128:C_M128 + 512] = (q % 128 != 0)[None, :]
    j = np.arange(32)
    c[0:32, C_C32:C_C32 + 32] = (j[None, :] >= j[:, None])
    for r in range(8):
        c[r, C_SELB + 128 * r:C_SELB + 128 * (r + 1)] = 1.0
    c[:, C_INV] = (10000.0 ** (-(np.arange(0, 32, 2, dtype=np.float32)) / 32.0)).astype(np.float32)[i % 16]
    c[:, C_ONE:C_ONE + 128] = 1.0
    return c


def build(T, L, with_b=True, with_c=True, dbg=False, only=None):
    NT = T // TT
    NB = T // 128
    nc = bass.Bass("TRN2", target_bir_lowering=False)
    kb = KB(nc)

    def dram_in(name, shape, dt=F32):
        return nc.dram_tensor(name, list(shape), dt, kind="ExternalInput").ap()

    def dram_sc(name, shape, dt):
        return nc.dram_tensor(name, list(shape), dt, kind="Internal").ap()

    xT_in = dram_in("xT", [D, T])
    pT_in = dram_in("pT", [L, PLE, T])
    pos_in = dram_in("pos", [1, T], I32)
    cst_in = dram_in("cst", [128, C_END])
    w_in = dram_in("w_in", [L, D, INW])
    w_uqn = dram_in("w_uqn", [L, 256, 512])
    w_uqr = dram_in("w_uqr", [L, 256, 256])
    w_uk = dram_in("w_uk", [L, 128, 512])
    w_uv = dram_in("w_uv", [L, 128, 512])
    colsA = dram_in("colsA", [L, 128, 16])
    colsB = dram_in("colsB", [L, 128, 64])
    colsC = dram_in("colsC", [L, 128, 32])
    colsW = dram_in("colsW", [L, 128, 12, 4])
    colsG = dram_in("colsG", [L, 128, 8])
    w_br = dram_in("w_br", [L, 3, 512, D])
    w_out = dram_in("w_out", [L, D, D])
    ple_proj = dram_in("ple_proj", [L, PLE, D])
    ple_gate = dram_in("ple_gate", [L, D, D])
    outT = nc.dram_tensor("outT", [D, T], F32, kind="ExternalOutput").ap()

    XT = dram_sc("XT", [D, T], F32)
    COSd = dram_sc("COSd", [128, T], F32)
    SINd = dram_sc("SINd", [128, T], F32)
    Qd = dram_sc("Qd", [8, 96, T], BF16)
    Kd = dram_sc("Kd", [8, 64, T], BF16)
    KRd = dram_sc("KRd", [32, T], BF16)
    Vd = dram_sc("Vd", [8, 128, NB, 64], BF16)
    XTb = dram_sc("XTb", [D, T], BF16)
    MGd = dram_sc("MGd", [D, T], BF16)
    SZAd = dram_sc("SZAd", [512, T], F32)
    Yd = dram_sc("Yd", [3, 512, T], BF16)
    if dbg:
        dbg_y = nc.dram_tensor("dbg_y", [3, 512, T], BF16, kind="ExternalOutput").ap()
        dbg_s = nc.dram_tensor("dbg_s", [128, 8], F32, kind="ExternalOutput").ap()

    pid = [0]

    def phase(fn, name=None):
        if only is not None and name not in only:
            return
        with ExitStack() as st:
            kb.begin()
            cnt = [0]
            pid[0] += 1
            ph = pid[0]

            def sb(shape, dt, nb=1):
                cnt[0] += 1
                return Tile(st.enter_context(nc.sbuf_tensor("t%d_%d" % (ph, cnt[0]), list(shape), dt)), nb)

            def psums():
                return [Tile(st.enter_context(nc.psum_tensor("ps%d_%d" % (ph, i), [128, 512], F32))) for i in range(8)]

            fn(sb, psums)
            kb.end()

    def fsz(ap):
        n = 1
        for d in ap.shape[1:]:
            n *= d
        return n

    def mm(out_t, out_ap, lhsT, rhs, reads, start=True, stop=True):
        c = max(32.0, fsz(rhs) * 0.42) * (8.0 if rhs.dtype == F32 else 1.0)
        kb.op("pe", lambda e: e.matmul(out_ap, lhsT, rhs, start=start, stop=stop), reads, [out_t.b], cost=c, lat=120.0)

    def act(out_ap, in_ap, func, reads, writes, small=False, **kw):
        kb.op("act", lambda e: e.activation(out_ap, in_ap, func, **kw), reads, writes, small=small, cost=220.0 + 0.58 * fsz(out_ap), lat=60.0, tab=TABS.get(func))

    def tt(eng, out_ap, in0, in1, op, reads, writes, small=False):
        kb.op(eng, lambda e: e.tensor_tensor(out_ap, in0, in1, op), reads, writes, small=small, cost=(70.0 + 1.05 * fsz(out_ap)) if eng != "pool" else (100.0 + 2.2 * fsz(out_ap)), lat=60.0)

    def ts(eng, out_ap, in0, s1, s2, op0, op1, reads, writes, small=False):
        if op1 is None:
            kb.op(eng, lambda e: e.tensor_scalar(out_ap, in0, s1, None, op0), reads, writes, small=small, cost=(70.0 + 1.05 * fsz(out_ap)) if eng != "pool" else 7400.0, lat=60.0)
        else:
            kb.op(eng, lambda e: e.tensor_scalar(out_ap, in0, s1, s2, op0, op1), reads, writes, small=small, cost=(70.0 + 1.05 * fsz(out_ap)) if eng != "pool" else 7400.0, lat=60.0)

    def stt(eng, out_ap, in0, s, in1, op0, op1, reads, writes):
        kb.op(eng, lambda e: e.scalar_tensor_tensor(out_ap, in0, s, in1, op0, op1), reads, writes, cost=70.0 + 1.05 * fsz(out_ap), lat=60.0)

    def cp(eng, out_ap, in_ap, reads, writes, small=False):
        kb.op(eng, lambda e: e.tensor_copy(out_ap, in_ap), reads, writes, small=small, cost=(70.0 + 0.8 * fsz(out_ap)) if eng != "pool" else (100.0 + 3.4 * fsz(out_ap)), lat=60.0)

    class Rot:
        def __init__(self, items):
            self.items = items
            self.i = 0

        def next(self):
            x = self.items[self.i % len(self.items)]
            self.i += 1
            return x

    def load_consts(sb, names):
        out = {}
        dsem = kb.new_dsem()
        for nm, c0, c1, dt, rows in names:
            f = sb([128, c1 - c0], F32)
            kb.dma(f.t[0:rows, :], cst_in[0:rows, c0:c1], [], [f.b], dsem)
            if dt == BF16:
                g = sb([128, c1 - c0], BF16)
                cp("dve", g.t[0:rows, :], f.t[0:rows, :], [f.b], [g.b])
                out[nm] = g
            else:
                out[nm] = f
        return out

    def load_w_bf16(dst, dst_ap_fn, src_ap_fn, nchunks, width, stage, sdsem, engs):
        for kc in range(nchunks):
            s = stage.next()
            dsm = sdsem[id(s)]
            kb.dma(s.t[:, 0:width], src_ap_fn(kc), [], [s.b], dsm)
            eng = engs.next()
            if eng == "act":
                act(dst_ap_fn(kc), s.t[:, 0:width], AF.Copy, [s.b], [dst.b])
            else:
                cp(eng, dst_ap_fn(kc), s.t[:, 0:width], [s.b], [dst.b])

    def p0(sb, psums):
        cs = load_consts(sb, [("inv", C_INV, C_INV + 1, F32, 128)])
        inv = cs["inv"]
        TWO_PI = 2.0 * math.pi
        C1 = 6.28125
        C2 = TWO_PI - C1
        PI_S = 3.1415925
        pi_ = [sb([128, TT], I32) for _ in range(2)]
        pi_d = [kb.new_dsem() for _ in range(2)]
        ang = sb([128, TT], F32)
        a2 = Rot([sb([128, TT], F32) for _ in range(2)])
        ki = sb([128, TT], I32)
        kf = sb([128, TT], F32)
        sn = [sb([128, TT], F32) for _ in range(2)]
        sn_d = [kb.new_dsem() for _ in range(2)]
        xf = [sb([128, 8, TT], F32) for _ in range(2)]
        xf_d = [kb.new_dsem() for _ in range(2)]
        xo = [sb([128, 8, TT], BF16) for _ in range(2)]
        xo_d = [kb.new_dsem() for _ in range(2)]
        for t in range(NT):
            i = t % 2
            sl = slice(t * TT, (t + 1) * TT)
            kb.dma(pi_[i].t[:], pos_in[0:1, sl].broadcast_to([128, TT]), [], [pi_[i].b], pi_d[i])
            kb.dma(xf[i].t[:], xT_in[:, sl].rearrange("(c p) t -> p c t", p=128), [], [xf[i].b], xf_d[i])
            cp("dve", ang.t[:], pi_[i].t[:], [pi_[i].b], [ang.b])
            ts("dve", ang.t[:], ang.t[:], inv.t[:, 0:1], None, ALU.mult, None, [ang.b, inv.b], [ang.b])
            for k, (shift, dst) in enumerate(((0.0, SINd), (0.5 * math.pi, COSd))):
                a = a2.next()
                ts("dve", a.t[:], ang.t[:], shift, None, ALU.add, None, [ang.b], [a.b])
                ts("dve", ki.t[:], a.t[:], 1.0 / TWO_PI, None, ALU.mult, None, [a.b], [ki.b])
                cp("dve", kf.t[:], ki.t[:], [ki.b], [kf.b])
                stt("dve", a.t[:], kf.t[:], -C1, a.t[:], ALU.mult, ALU.add, [kf.b, a.b], [a.b])
                stt("dve", a.t[:], kf.t[:], -C2, a.t[:], ALU.mult, ALU.add, [kf.b, a.b], [a.b])
                ts("dve", a.t[:], a.t[:], -PI_S, PI_S, ALU.max, ALU.min, [a.b], [a.b])
                act(sn[k].t[:], a.t[:], AF.Sin, [a.b], [sn[k].b])
                kb.dma(dst[:, sl], sn[k].t[:], [sn[k].b], [Buf()], sn_d[k])
            for kc in range(8):
                cp(("pool", "act")[kc % 2] if False else "pool", xo[i].t[:, kc, :], xf[i].t[:, kc, :], [xf[i].b], [xo[i].b])
            kb.dma(XTb[:, sl].rearrange("(c p) t -> p c t", p=128), xo[i].t[:], [xo[i].b], [Buf()], xo_d[i])

    def p1a(l, sb, psums):
        ps = psums()
        pr = Rot(ps)
        cs = load_consts(sb, [("one", C_ONE, C_ONE + 128, BF16, 128)])
        ones = cs["one"]
        NA = 928
        Win = sb([128, 8, NA + 32], BF16)
        stage = Rot([sb([128, 1024], F32) for _ in range(3)])
        sdsem = {id(s): kb.new_dsem() for s in stage.items}
        engs = Rot(["pool", "dve", "act"])
        load_w_bf16(Win, lambda kc: Win.t[:, kc, 0:NA], lambda kc: w_in[l, kc * 128:(kc + 1) * 128, 0:NA], 8, NA, stage, sdsem, engs)
        for kc in range(8):
            ts("dve", Win.t[:, kc, NA:NA + 16], Win.t[:, kc, O_KR + 16:O_KR + 32], -1.0, None, ALU.mult, None, [Win.b], [Win.b])
            cp("dve", Win.t[:, kc, NA + 16:NA + 32], Win.t[:, kc, O_KR:O_KR + 16], [Win.b], [Win.b])
        Wqn = sb([128, 2, 512], BF16)
        load_w_bf16(Wqn, lambda kc: Wqn.t[:, kc, :], lambda kc: w_uqn[l, kc * 128:(kc + 1) * 128, :], 2, 512, stage, sdsem, engs)
        Wqr = sb([128, 2, 256], BF16)
        load_w_bf16(Wqr, lambda kc: Wqr.t[:, kc, :], lambda kc: w_uqr[l, kc * 128:(kc + 1) * 128, :], 2, 256, stage, sdsem, engs)
        Wqx = sb([128, 2, 256], BF16)
        for kc in range(2):
            for h in range(8):
                ts("dve", Wqx.t[:, kc, 32 * h:32 * h + 16], Wqr.t[:, kc, 32 * h + 16:32 * h + 32], -1.0, None, ALU.mult, None, [Wqr.b], [Wqx.b])
                cp("dve", Wqx.t[:, kc, 32 * h + 16:32 * h + 32], Wqr.t[:, kc, 32 * h:32 * h + 16], [Wqr.b], [Wqx.b])
        Wk = sb([128, 512], BF16)
        load_w_bf16(Wk, lambda kc: Wk.t[:, :], lambda kc: w_uk[l, :, :], 1, 512, stage, sdsem, engs)
        Wv = sb([128, 512], BF16)
        load_w_bf16(Wv, lambda kc: Wv.t[:, :], lambda kc: w_uv[l, :, :], 1, 512, stage, sdsem, engs)
        colA = sb([128, 16], F32)
        kb.dma(colA.t[:], colsA[l], [], [colA.b], kb.new_dsem())
        eps_t = sb([128, 1], F32)
        kb.op("pool", lambda e: e.memset(eps_t.t[:], EPS), [], [eps_t.b])

        xb = [sb([128, 8, TT], BF16) for _ in range(2)]
        xb_d = [kb.new_dsem() for _ in range(2)]
        cosb = [sb([128, TT], F32) for _ in range(2)]
        sinb = [sb([128, TT], F32) for _ in range(2)]
        cs_d = [kb.new_dsem() for _ in range(2)]
        cosS = sb([128, TT], F32)
        sinS = sb([128, TT], F32)
        xs = sb([128, 2, TT], F32)
        sq = sb([128, 2, TT], BF16)
        rstd = sb([128, TT], F32)
        cqn = sb([128, 2, TT], BF16)
        ckvn = sb([128, TT], BF16)
        ob16 = Rot([sb([128, TT], BF16) for _ in range(6)])
        of32 = Rot([sb([128, TT], F32) for _ in range(6)])
        st_d = {}

        def store(dst_ap, tile, src_ap):
            d = st_d.setdefault(id(tile), kb.new_dsem())
            kb.dma(dst_ap, src_ap, [tile.b], [Buf()], d)

        def issue_loads(t):
            i = t % 2
            sl = slice(t * TT, (t + 1) * TT)
            kb.dma(xb[i].t[:], XTb[:, sl].rearrange("(c p) t -> p c t", p=128), [], [xb[i].b], xb_d[i])
            kb.dma(cosb[i].t[:], COSd[:, sl], [], [cosb[i].b], cs_d[i])
            kb.dma(sinb[i].t[:], SINd[:, sl], [], [sinb[i].b], cs_d[i])

        def rms_feat(chunks_ps, n_feat, gain_cols, out_t, out_aps):
            nck = len(chunks_ps)
            for c, p in enumerate(chunks_ps):
                act(xs.t[:, c, :], p.t[:], AF.Copy, [p.b], [xs.b])
                act(sq.t[:, c, :], p.t[:], AF.Square, [p.b], [sq.b])
            ss = pr.next()
            for c in range(nck):
                mm(ss, ss.t[:], ones.t[:, :], sq.t[:, c, :], [ones.b, sq.b], start=(c == 0), stop=(c == nck - 1))
            act(rstd.t[:], ss.t[:], AF.Ln, [ss.b, eps_t.b], [rstd.b], bias=eps_t.t[:, 0:1], scale=1.0 / n_feat)
            act(rstd.t[:], rstd.t[:], AF.Exp, [rstd.b], [rstd.b], scale=-0.5)
            for c in range(nck):
                stt("dve", out_aps[c], xs.t[:, c, :], gain_cols[c], rstd.t[:], ALU.mult, ALU.mult, [xs.b, rstd.b, colA.b], [out_t.b])

        issue_loads(0)
        for t in range(NT):
            i = t % 2
            sl = slice(t * TT, (t + 1) * TT)
            if t + 1 < NT:
                issue_loads(t + 1)
            X = xb[i]

            def proj(cols, M):
                p = pr.next()
                for kc in range(8):
                    mm(p, p.t[0:M, :], Win.t[:, kc, cols:cols + M], X.t[:, kc, :], [Win.b, X.b], start=(kc == 0), stop=(kc == 7))
                return p

            act(cosS.t[:], cosb[i].t[:], AF.Copy, [cosb[i].b], [cosS.b], scale=A_SCALE)
            act(sinS.t[:], sinb[i].t[:], AF.Copy, [sinb[i].b], [sinS.b], scale=A_SCALE)
            cq_ps = [proj(O_CQ + 128 * c, 128) for c in range(2)]
            rms_feat(cq_ps, 256, [colA.t[:, 0:1], colA.t[:, 1:2]], cqn, [cqn.t[:, 0, :], cqn.t[:, 1, :]])
            for ch in range(4):
                p = pr.next()
                for kc in range(2):
                    mm(p, p.t[:], Wqn.t[:, kc, ch * 128:(ch + 1) * 128], cqn.t[:, kc, :], [Wqn.b, cqn.b], start=(kc == 0), stop=(kc == 1))
                o = ob16.next()
                act(o.t[:], p.t[:], AF.Copy, [p.b], [o.b], scale=A_SCALE)
                for hh in range(2):
                    store(Qd[2 * ch + hh, 0:64, sl], o, o.t[64 * hh:64 * hh + 64, :])
            for ch in range(2):
                p1_ = pr.next()
                for kc in range(2):
                    mm(p1_, p1_.t[:], Wqr.t[:, kc, ch * 128:(ch + 1) * 128], cqn.t[:, kc, :], [Wqr.b, cqn.b], start=(kc == 0), stop=(kc == 1))
                p2_ = pr.next()
                for kc in range(2):
                    mm(p2_, p2_.t[:], Wqx.t[:, kc, ch * 128:(ch + 1) * 128], cqn.t[:, kc, :], [Wqx.b, cqn.b], start=(kc == 0), stop=(kc == 1))
                t1 = of32.next()
                t2 = of32.next()
                tt("dve", t1.t[:], p1_.t[:], cosS.t[:], ALU.mult, [p1_.b, cosS.b], [t1.b])
                tt("dve", t2.t[:], p2_.t[:], sinS.t[:], ALU.mult, [p2_.b, sinS.b], [t2.b])
                o = ob16.next()
                tt("pool", o.t[:], t1.t[:], t2.t[:], ALU.add, [t1.b, t2.b], [o.b])
                for hh in range(4):
                    store(Qd[4 * ch + hh, 64:96, sl], o, o.t[32 * hh:32 * hh + 32, :])
            ckv_ps = [proj(O_CKV, 128)]
            rms_feat(ckv_ps, 128, [colA.t[:, 2:3]], ckvn, [ckvn.t[:, :]])
            for ch in range(4):
                p = pr.next()
                mm(p, p.t[:], Wk.t[:, ch * 128:(ch + 1) * 128], ckvn.t[:, :], [Wk.b, ckvn.b])
                o = ob16.next()
                if ch % 2 == 0:
                    act(o.t[:], p.t[:], AF.Copy, [p.b], [o.b])
                else:
                    cp("dve", o.t[:], p.t[:], [p.b], [o.b])
                for hh in range(2):
                    store(Kd[2 * ch + hh, :, sl], o, o.t[64 * hh:64 * hh + 64, :])
            for blk in range(4):
                p = pr.next()
                mm(p, p.t[:], ckvn.t[:, blk * 128:(blk + 1) * 128], Wv.t[:, :], [Wv.b, ckvn.b])
                o = ob16.next()
                if blk % 2 == 0:
                    act(o.t[:], p.t[:], AF.Copy, [p.b], [o.b])
                else:
                    cp("dve", o.t[:], p.t[:], [p.b], [o.b])
                store(Vd[:, :, t * 4 + blk, :].rearrange("h p c -> p h c"), o, o.t[:, :].rearrange("p (h c) -> p h c", c=64))
            pk1 = proj(O_KR, 32)
            pk2 = proj(NA, 32)
            t1 = of32.next()
            t2 = of32.next()
            tt("dve", t1.t[0:32, :], pk1.t[0:32, :], cosb[i].t[0:32, :], ALU.mult, [pk1.b, cosb[i].b], [t1.b])
            tt("dve", t2.t[0:32, :], pk2.t[0:32, :], sinb[i].t[0:32, :], ALU.mult, [pk2.b, sinb[i].b], [t2.b])
            o = ob16.next()
            tt("pool", o.t[0:32, :], t1.t[0:32, :], t2.t[0:32, :], ALU.add, [t1.b, t2.b], [o.b])
            store(KRd[:, sl], o, o.t[0:32, :])
            for c in range(4):
                p = proj(O_ZA + 128 * c, 128)
                o = of32.next()
                act(o.t[:], p.t[:], AF.Silu, [p.b], [o.b])
                store(SZAd[c * 128:(c + 1) * 128, sl], o, o.t[:, :])

    def p2(l, sb, psums):
        ps = psums()
        s_rot = Rot(ps[0:5])
        o_rot = Rot(ps[5:8])
        cs = load_consts(sb, [("cm", C_CM, C_CM + 2048, BF16, 128)])
        cm = cs["cm"]
        Ksb = [sb([96, T], BF16) for _ in range(2)]
        Vraw = [sb([128, NB * 64], BF16) for _ in range(2)]
        Vsb = [sb([128, NB, 128], BF16) for _ in range(2)]
        kv_d = [kb.new_dsem() for _ in range(2)]
        for i in range(2):
            kb.op("pool", lambda e, i=i: e.memset(Vsb[i].t[:, :, 64:128], 1.0), [], [Vsb[i].b])
        NQ = 4
        Qsb = [sb([96, TT], BF16) for _ in range(NQ)]
        Zsb = [sb([64, TT], F32) for _ in range(NQ)]
        q_d = [kb.new_dsem() for _ in range(NQ)]
        Psb = Rot([sb([128, TT], BF16) for _ in range(6)])
        rsb = Rot([sb([64, TT], F32) for _ in range(2)])
        osb = Rot([sb([64, TT], F32) for _ in range(2)])
        ysb = Rot([sb([64, TT], BF16) for _ in range(2)])
        y_d = {}

        def load_kv(h):
            i = h % 2
            kb.dma(Ksb[i].t[0:64, :], Kd[h, :, :], [], [Ksb[i].b], kv_d[i])
            kb.dma(Ksb[i].t[64:96, :], KRd[:, :], [], [Ksb[i].b], kv_d[i])
            kb.dma(Vraw[i].t[:, :], Vd[h].rearrange("p b c -> p (b c)"), [], [Vraw[i].b], kv_d[i])

        seq = [(h, qt) for h in range(8) for qt in range(NT)]

        def load_q(n):
            h, qt = seq[n]
            j = n % NQ
            sl = slice(qt * TT, (qt + 1) * TT)
            kb.dma(Qsb[j].t[:, :], Qd[h, :, sl], [], [Qsb[j].b], q_d[j])
            kb.dma(Zsb[j].t[:, :], SZAd[h * 64:(h + 1) * 64, sl], [], [Zsb[j].b], q_d[j])

        work = []
        for n, (h, qt) in enumerate(seq):
            nkb = 4 * (qt + 1)
            for kbk in range(nkb):
                work.append((n, h, qt, kbk, nkb))
        LA = 3
        load_kv(0)
        load_q(0)
        if len(seq) > 1:
            load_q(1)
        pts = {}
        obank = {}
        for idx in range(len(work) + LA):
            if idx < len(work):
                n, h, qt, kbk, nkb = work[idx]
                i = h % 2
                j = n % NQ
                if kbk == 0:
                    if qt == 0:
                        cp("pool", Vsb[i].t[:, :, 0:64], Vraw[i].t[:, :].rearrange("p (b c) -> p b c", c=64), [Vraw[i].b], [Vsb[i].b])
                        if h + 1 < 8:
                            load_kv(h + 1)
                    if n + 2 < len(seq):
                        load_q(n + 2)
                s_ = s_rot.next()
                mm(s_, s_.t[:], Ksb[i].t[:, kbk * 128:(kbk + 1) * 128], Qsb[j].t[:, :], [Ksb[i].b, Qsb[j].b])
                pt = Psb.next()
                act(pt.t[:], s_.t[:], AF.Exp, [s_.b], [pt.b])
                m = kbk - 4 * qt
                if m >= 0:
                    tt("pool", pt.t[:], pt.t[:], cm.t[:, m * 512:(m + 1) * 512], ALU.mult, [pt.b, cm.b], [pt.b])
                pts[idx] = pt
            k2 = idx - LA
            if k2 >= 0:
                n, h, qt, kbk, nkb = work[k2]
                i = h % 2
                j = n % NQ
                sl = slice(qt * TT, (qt + 1) * TT)
                if kbk == 0:
                    obank[n] = o_rot.next()
                ops_ = obank[n]
                pt = pts.pop(k2)
                mm(ops_, ops_.t[:], Vsb[i].t[:, kbk, :], pt.t[:], [Vsb[i].b, pt.b], start=(kbk == 0), stop=(kbk == nkb - 1))
                if kbk == nkb - 1:
                    r = rsb.next()
                    kb.op("dve", lambda e, r=r, ops_=ops_: e.reciprocal(r.t[:, :], ops_.t[64:128, :]), [ops_.b], [r.b])
                    o = osb.next()
                    tt("dve", o.t[:, :], ops_.t[0:64, :], r.t[:, :], ALU.mult, [ops_.b, r.b], [o.b])
                    y = ysb.next()
                    tt("pool", y.t[:, :], o.t[:, :], Zsb[j].t[:, :], ALU.mult, [o.b, Zsb[j].b], [y.b])
                    kb.dma(Yd[0, h * 64:(h + 1) * 64, sl], y.t[:, :], [y.b], [Buf()])
                    del obank[n]

    def pzero(n, sb, psums):
        z = sb([128, TT], BF16)
        kb.op("pool", lambda e: e.memset(z.t[:], 0.0), [], [z.b])
        d = kb.new_dsem()
        for t in range(NT):
            for c in range(4):
                kb.dma(Yd[n, c * 128:(c + 1) * 128, t * TT:(t + 1) * TT], z.t[:], [z.b], [Buf()], d)

    def p1c(l, sb, psums):
        ps = psums()
        pr = Rot(ps)
        cs = load_consts(sb, [("one", C_ONE, C_ONE + 128, BF16, 128), ("id", C_ID, C_ID + 128, BF16, 128),
                              ("m32", C_M32, C_M32 + 512, F32, 128), ("c32", C_C32, C_C32 + 32, F32, 32)])
        ones, ident, m32, c32 = cs["one"], cs["id"], cs["m32"], cs["c32"]
        NC_ = 2048
        Win = sb([128, 8, NC_], BF16)
        stage = Rot([sb([128, 1024], F32) for _ in range(3)])
        sdsem = {id(s): kb.new_dsem() for s in stage.items}
        engs = Rot(["pool", "dve", "act"])
        for part in range(2):
            load_w_bf16(Win, lambda kc, part=part: Win.t[:, kc, part * 1024:(part + 1) * 1024],
                        lambda kc, part=part: w_in[l, kc * 128:(kc + 1) * 128, O_HQ + part * 1024:O_HQ + (part + 1) * 1024], 8, 1024, stage, sdsem, engs)
        colC = sb([128, 32], F32)
        kb.dma(colC.t[:], colsC[l], [], [colC.b], kb.new_dsem())
        eps_t = sb([128, 1], F32)
        kb.op("pool", lambda e: e.memset(eps_t.t[:], EPS), [], [eps_t.b])
        ex = sb([128, 16], F32)
        act(ex.t[:], colC.t[:, 8:24], AF.Exp, [colC.b], [ex.b], small=True)
        exv = ex.t[:, :].rearrange("p (h j) -> p h j", j=4)
        ssum = sb([128, 4], F32)
        lbt = sb([128, 4], F32)
        oml = sb([128, 4], F32)
        tt("dve", ssum.t[:], exv[:, :, 0], exv[:, :, 1], ALU.add, [ex.b], [ssum.b], small=True)
        tt("dve", ssum.t[:], ssum.t[:], exv[:, :, 2], ALU.add, [ex.b, ssum.b], [ssum.b], small=True)
        tt("dve", ssum.t[:], ssum.t[:], exv[:, :, 3], ALU.add, [ex.b, ssum.b], [ssum.b], small=True)
        kb.op("dve", lambda e: e.memset(lbt.t[:], 0.0), [], [lbt.b], small=True)
        for j in range(1, l + 1):
            tt("dve", lbt.t[:], lbt.t[:], exv[:, :, j], ALU.add, [ex.b, lbt.b], [lbt.b], small=True)
        kb.op("dve", lambda e: e.reciprocal(ssum.t[:], ssum.t[:]), [ssum.b], [ssum.b], small=True)
        tt("dve", lbt.t[:], lbt.t[:], ssum.t[:], ALU.mult, [lbt.b, ssum.b], [lbt.b], small=True)
        ts("dve", lbt.t[:], lbt.t[:], 0.0, 1.0 - 1e-6, ALU.max, ALU.min, [lbt.b], [lbt.b], small=True)
        ts("dve", oml.t[:], lbt.t[:], -1.0, 1.0, ALU.mult, ALU.add, [lbt.b], [oml.b], small=True)
        hom = sb([128, 4], F32)
        ts("dve", hom.t[:], oml.t[:], 0.5, None, ALU.mult, None, [oml.b], [hom.b], small=True)

        if dbg:
            kb.dma(dbg_s[:, 0:4], lbt.t[:, :], [lbt.b], [Buf()], kb.new_dsem())
            kb.dma(dbg_s[:, 4:8], oml.t[:, :], [oml.b], [Buf()], kb.new_dsem())
        xb = [sb([128, 8, TT], BF16) for _ in range(2)]
        xb_d = [kb.new_dsem() for _ in range(2)]
        S = sb([128, 4, 128], F32)
        kb.op("pool", lambda e: e.memset(S.t[:], 0.0), [], [S.b])
        Sb = Rot([sb([128, 128], BF16) for _ in range(6)])
        f32r = Rot([sb([128, TT], F32) for _ in range(14)])
        b16r = Rot([sb([128, TT], BF16) for _ in range(8)])
        tmr = Rot([sb([32, 16, 128], BF16) for _ in range(4)])
        scr = Rot([sb([32, TT], BF16) for _ in range(2)])
        yr = Rot([sb([128, TT], BF16) for _ in range(2)])
        st_d = {}

        def store(dst_ap, tile, src_ap):
            d = st_d.setdefault(id(tile), kb.new_dsem())
            kb.dma(dst_ap, src_ap, [tile.b], [Buf()], d)

        def issue_loads(t):
            i = t % 2
            sl = slice(t * TT, (t + 1) * TT)
            kb.dma(xb[i].t[:], XTb[:, sl].rearrange("(c p) t -> p c t", p=128), [], [xb[i].b], xb_d[i])

        issue_loads(0)
        for t in range(NT):
            i = t % 2
            sl = slice(t * TT, (t + 1) * TT)
            if t + 1 < NT:
                issue_loads(t + 1)
            X = xb[i]

            def proj(cols):
                p = pr.next()
                for kc in range(8):
                    mm(p, p.t[:], Win.t[:, kc, cols:cols + 128], X.t[:, kc, :], [Win.b, X.b], start=(kc == 0), stop=(kc == 7))
                return p

            for h in range(4):
                hq_ps = proj(128 * h)
                qf = f32r.next()
                act(qf.t[:], hq_ps.t[:], AF.Silu, [hq_ps.b], [qf.b])
                hf_ps = proj(512 + 128 * h)
                sig = f32r.next()
                act(sig.t[:], hf_ps.t[:], AF.Tanh, [hf_ps.b], [sig.b], scale=0.5)
                hi_ps = proj(1024 + 128 * h)
                vT = b16r.next()
                act(vT.t[:], hi_ps.t[:], AF.Copy, [hi_ps.b], [vT.b])
                zc_ps = proj(1536 + 128 * h)
                szc = f32r.next()
                act(szc.t[:], zc_ps.t[:], AF.Silu, [zc_ps.b], [szc.b])
                tq = f32r.next()
                ts("dve", tq.t[:], sig.t[:], hom.t[:, h:h + 1], hom.t[:, h:h + 1], ALU.mult, ALU.add, [sig.b, hom.b], [tq.b])
                ff = f32r.next()
                ts("dve", ff.t[:], tq.t[:], 1e-30, lbt.t[:, h:h + 1], ALU.max, ALU.add, [tq.b, lbt.b], [ff.b])
                act(ff.t[:], ff.t[:], AF.Ln, [ff.b], [ff.b])
                kk = f32r.next()
                ts("dve", kk.t[:], tq.t[:], -1.0, oml.t[:, h:h + 1], ALU.mult, ALU.add, [tq.b, oml.b], [kk.b])
                cum = f32r.next()
                kb.op("dve", lambda e, cum=cum, ff=ff: e.tensor_tensor_scan(cum.t[:], m32.t[:], ff.t[:], 0.0, ALU.mult, ALU.add), [m32.b, ff.b], [cum.b])
                ecum = f32r.next()
                act(ecum.t[:], cum.t[:], AF.Exp, [cum.b], [ecum.b])
                einv = f32r.next()
                act(einv.t[:], cum.t[:], AF.Exp, [cum.b], [einv.b], scale=-1.0)
                qdec = b16r.next()
                stt("dve", qdec.t[:], qf.t[:], 128 ** -0.5, ecum.t[:], ALU.mult, ALU.mult, [qf.b, ecum.b], [qdec.b])
                kinv = b16r.next()
                tt("dve", kinv.t[:], kk.t[:], einv.t[:], ALU.mult, [kk.b, einv.b], [kinv.b])
                dd = f32r.next()
                cv = cum.t[:, :].rearrange("p (c t) -> p c t", t=32)
                tt("pool", dd.t[:, :].rearrange("p (c t) -> p c t", t=32), cv[:, :, 31:32].broadcast_to([128, 16, 32]), cv, ALU.subtract, [cum.b], [dd.b])
                act(dd.t[:], dd.t[:], AF.Exp, [dd.b], [dd.b])
                kdT = b16r.next()
                tt("pool", kdT.t[:], kk.t[:], dd.t[:], ALU.mult, [kk.b, dd.b], [kdT.b])
                kd_tm = tmr.next()
                v_tm = tmr.next()
                for src, dst in ((kdT, kd_tm), (vT, v_tm)):
                    for half in range(2):
                        p = pr.next()
                        pv = p.t[:].bitcast(BF16)
                        for c in range(8):
                            cc = half * 8 + c
                            kb.op("pe", lambda e, pv=pv, src=src, cc=cc, c=c: e.transpose(pv[0:32, c * 128:(c + 1) * 128], src.t[:, cc * 32:(cc + 1) * 32], ident.t[:, :]),
                                  [src.b, ident.b], [p.b])
                        dv = dst.t[:, half * 8:(half + 1) * 8, :].rearrange("p c d -> p (c d)")
                        if half == 0:
                            act(dv, pv[0:32, :], AF.Copy, [p.b], [dst.b])
                        else:
                            cp("dve", dv, pv[0:32, :], [p.b], [dst.b])
                sc_ps = pr.next()
                for c in range(16):
                    mm(sc_ps, sc_ps.t[0:32, c * 32:(c + 1) * 32], kinv.t[:, c * 32:(c + 1) * 32], qdec.t[:, c * 32:(c + 1) * 32], [kinv.b, qdec.b])
                scm = scr.next()
                tt("dve", scm.t[:, :].rearrange("p (c t) -> p c t", t=32), sc_ps.t[0:32, :].rearrange("p (c t) -> p c t", t=32),
                   c32.t[0:32, :].unsqueeze(1).broadcast_to([32, 16, 32]), ALU.mult, [sc_ps.b, c32.b], [scm.b])
                o_ps = pr.next()
                dS = None
                for c in range(16):
                    if c % 4 == 0:
                        dS = pr.next()
                        for c2 in range(c, c + 4):
                            mm(dS, dS.t[:, (c2 % 4) * 128:(c2 % 4 + 1) * 128], kd_tm.t[:, c2, :], v_tm.t[:, c2, :], [kd_tm.b, v_tm.b])
                    sbf = Sb.next()
                    cp("dve", sbf.t[:, :], S.t[:, h, :], [S.b], [sbf.b])
                    mm(o_ps, o_ps.t[:, c * 32:(c + 1) * 32], sbf.t[:, :], qdec.t[:, c * 32:(c + 1) * 32], [sbf.b, qdec.b], start=True, stop=False)
                    mm(o_ps, o_ps.t[:, c * 32:(c + 1) * 32], v_tm.t[:, c, :], scm.t[:, c * 32:(c + 1) * 32], [v_tm.b, scm.b], start=False, stop=True)
                    stt("dve", S.t[:, h, :], S.t[:, h, :], ecum.t[:, c * 32 + 31:c * 32 + 32], dS.t[:, (c % 4) * 128:(c % 4 + 1) * 128], ALU.mult, ALU.add,
                        [S.b, ecum.b, dS.b], [S.b])
                osb = f32r.next()
                act(osb.t[:], o_ps.t[:], AF.Copy, [o_ps.b], [osb.b])
                sq = b16r.next()
                act(sq.t[:], o_ps.t[:], AF.Square, [o_ps.b], [sq.b])
                ss = pr.next()
                mm(ss, ss.t[:], ones.t[:, :], sq.t[:, :], [ones.b, sq.b])
                rstd = f32r.next()
                act(rstd.t[:], ss.t[:], AF.Ln, [ss.b, eps_t.b], [rstd.b], bias=eps_t.t[:, 0:1], scale=1.0 / 128)
                act(rstd.t[:], rstd.t[:], AF.Exp, [rstd.b], [rstd.b], scale=-0.5)
                stt("dve", osb.t[:], osb.t[:], colC.t[:, 0:1], rstd.t[:], ALU.mult, ALU.mult, [osb.b, colC.b, rstd.b], [osb.b])
                y = yr.next()
                tt("pool", y.t[:], osb.t[:], szc.t[:], ALU.mult, [osb.b, szc.b], [y.b])
                store(Yd[2, 128 * h:128 * (h + 1), sl], y, y.t[:, :])

    def p1b(l, sb, psums):
        ps = psums()
        pr = Rot(ps[0:6])
        cs = load_consts(sb, [("one", C_ONE, C_ONE + 128, BF16, 128), ("id", C_ID, C_ID + 128, BF16, 128),
                              ("idf", C_ID, C_ID + 128, F32, 128),
                              ("m128", C_M128, C_M128 + 512, F32, 128), ("uti", C_UTI, C_UTI + 128, F32, 128),
                              ("uts", C_UTS, C_UTS + 128, F32, 128), ("selb", C_SELB, C_SELB + 1024, F32, 8)])
        ones, ident, identf, m128, uti, uts, selb = cs["one"], cs["id"], cs["idf"], cs["m128"], cs["uti"], cs["uts"], cs["selb"]
        NBW = 2056
        Win = sb([128, 8, NBW], BF16)
        stage = Rot([sb([128, 1028], F32) for _ in range(2)])
        sdsem = {id(s): kb.new_dsem() for s in stage.items}
        engs = Rot(["pool", "dve", "act"])
        for part in range(2):
            load_w_bf16(Win, lambda kc, part=part: Win.t[:, kc, part * 1028:(part + 1) * 1028],
                        lambda kc, part=part: w_in[l, kc * 128:(kc + 1) * 128, O_QKV + part * 1028:O_QKV + (part + 1) * 1028], 8, 1028, stage, sdsem, engs)
        colC = sb([128, 32], F32)
        kb.dma(colC.t[:], colsC[l], [], [colC.b], kb.new_dsem())
        colW = sb([128, 12, 4], F32)
        kb.dma(colW.t[:], colsW[l], [], [colW.b], kb.new_dsem())
        colG = sb([128, 8], F32)
        kb.dma(colG.t[:], colsG[l], [], [colG.b], kb.new_dsem())
        eps_t = sb([128, 1], F32)
        kb.op("pool", lambda e: e.memset(eps_t.t[:], EPS), [], [eps_t.b])
        one_t = sb([128, 1], F32)
        kb.op("pool", lambda e: e.memset(one_t.t[:], 1.0), [], [one_t.b])
        lnq_t = sb([128, 1], F32)
        kb.op("pool", lambda e: e.memset(lnq_t.t[:], math.log(128 ** -0.5)), [], [lnq_t.b])
        nega = sb([128, 1], F32)
        act(nega.t[0:8, :], colG.t[0:8, 0:1], AF.Exp, [colG.b], [nega.b], small=True)
        ts("dve", nega.t[0:8, :], nega.t[0:8, :], -1.0, None, ALU.mult, None, [nega.b], [nega.b], small=True)

        xb = [sb([128, 8, TT], BF16) for _ in range(2)]
        xb_d = [kb.new_dsem() for _ in range(2)]
        raw = sb([128, 12, 4 + TT], BF16)
        rawb = [Buf() for _ in range(12)]
        for c in range(12):
            kb.op("pool", lambda e, c=c: e.memset(raw.t[:, c, 0:4], 0.0), [], [rawb[c]])
        Dg = sb([128, 12, 4, 128], BF16)
        for c in range(12):
            for j in range(4):
                ts("dve", Dg.t[:, c, j, :], ident.t[:, :], colW.t[:, c, j:j + 1], None, ALU.mult, None, [ident.b, colW.b], [Dg.b])
        qs = [sb([128, TT], F32) for _ in range(8)]
        vT = [sb([128, TT], BF16) for _ in range(4)]
        szb = [sb([128, TT], F32) for _ in range(4)]
        qnb = [sb([128, TT], BF16) for _ in range(4)]
        knb = [sb([128, TT], BF16) for _ in range(4)]
        x8 = sb([8, TT], F32)
        n8 = sb([8, TT], F32)
        G8 = sb([8, TT], F32)
        B8 = sb([8, TT], F32)
        gc8 = sb([8, TT], F32)
        gcT = sb([128, 4, 8], F32)
        bT = sb([128, 4, 8], F32)
        Sg = sb([128, 4, 128], F32)
        kb.op("pool", lambda e: e.memset(Sg.t[:], 0.0), [], [Sg.b])
        Sgb = [sb([128, 128], BF16) for _ in range(4)]
        for h in range(4):
            kb.op("pool", lambda e, h=h: e.memset(Sgb[h].t[:], 0.0), [], [Sgb[h].b])
        NH = 2
        def hset():
            d = {}
            for nm in ("kbT", "qdec", "U", "L", "U2", "L2", "P", "Q", "intraT", "kbg", "kdec", "vb", "wTn", "vn", "sq"):
                d[nm] = sb([128, TT], BF16)
            for nm in ("Es", "Ei", "egcB", "E"):
                d[nm] = sb([128, TT], F32)
            d["osb"] = d["E"]
            d["gcBs"] = sb([128, TT], F32)
            d["rstd"] = d["Es"]
            for nm in ("glast", "eglast", "egcT", "bgT", "kdcol"):
                d[nm] = sb([128, 4], F32)
            return d
        HS = [hset() for _ in range(NH)]
        t32 = Rot([sb([128, TT], F32) for _ in range(2)])
        yr = Rot([sb([128, TT], BF16) for _ in range(2)])
        st_d = {}

        def store(dst_ap, tile, src_ap):
            d = st_d.setdefault(id(tile), kb.new_dsem())
            kb.dma(dst_ap, src_ap, [tile.b], [Buf()], d)

        def issue_loads(t):
            i = t % 2
            sl = slice(t * TT, (t + 1) * TT)
            kb.dma(xb[i].t[:], XTb[:, sl].rearrange("(c p) t -> p c t", p=128), [], [xb[i].b], xb_d[i])

        def b4(tile_ap):
            return tile_ap.unsqueeze(1).broadcast_to([128, 4, 128])

        def v4(ap):
            return ap.rearrange("p (c t) -> p c t", t=128)

        import os
        KSTOP = os.environ.get('KSTOP', '')
        KD = int(os.environ.get('KD', '99'))
        issue_loads(0)
        for t in range(NT):
            i = t % 2
            sl = slice(t * TT, (t + 1) * TT)
            if t + 1 < NT:
                issue_loads(t + 1)
            X = xb[i]

            def proj(cols, M=128):
                p = pr.next()
                for kc in range(8):
                    mm(p, p.t[0:M, :], Win.t[:, kc, cols:cols + M], X.t[:, kc, :], [Win.b, X.b], start=(kc == 0), stop=(kc == 7))
                return p

            for c in range(12):
                p = proj(128 * c)
                if c % 2 == 0:
                    act(raw.t[:, c, 4:4 + TT], p.t[:], AF.Copy, [p.b], [rawb[c]])
                else:
                    cp("dve", raw.t[:, c, 4:4 + TT], p.t[:], [p.b], [rawb[c]])
                cv = pr.next()
                for j in range(4):
                    mm(cv, cv.t[:], Dg.t[:, c, j, :], raw.t[:, c, 1 + j:1 + j + TT], [Dg.b, rawb[c]], start=(j == 0), stop=(j == 3))
                cp("pool", raw.t[:, c, 0:4], raw.t[:, c, TT:TT + 4], [rawb[c]], [rawb[c]], small=True)
                if c < 8:
                    act(qs[c].t[:], cv.t[:], AF.Silu, [cv.b], [qs[c].b])
                else:
                    act(vT[c - 8].t[:], cv.t[:], AF.Silu, [cv.b], [vT[c - 8].b])
            for h in range(4):
                p = proj(1544 + 128 * h)
                act(szb[h].t[:], p.t[:], AF.Silu, [p.b], [szb[h].b])
            if KSTOP == 'A':
                break
            for c in range(8):
                sqt = HS[c % NH]["sq"]
                tt("pool", sqt.t[:], qs[c].t[:], qs[c].t[:], ALU.mult, [qs[c].b], [sqt.b])
                ss = pr.next()
                mm(ss, ss.t[:], ones.t[:, :], sqt.t[:, :], [ones.b, sqt.b])
                rs = t32.next()
                act(rs.t[:], ss.t[:], AF.Ln, [ss.b, eps_t.b], [rs.b], bias=eps_t.t[:, 0:1], scale=1.0)
                if c < 4:
                    act(rs.t[:], rs.t[:], AF.Exp, [rs.b, lnq_t.b], [rs.b], scale=-0.5, bias=lnq_t.t[:, 0:1])
                else:
                    act(rs.t[:], rs.t[:], AF.Exp, [rs.b], [rs.b], scale=-0.5)
                tt("dve", qs[c].t[:], qs[c].t[:], rs.t[:], ALU.mult, [qs[c].b, rs.b], [qs[c].b])
                dstb = qnb[c] if c < 4 else knb[c - 4]
                cp("pool", dstb.t[:], qs[c].t[:], [qs[c].b], [dstb.b])
            if KSTOP == 'B':
                break
            ab = proj(1536, 8)
            act(B8.t[:, :], ab.t[0:8, :], AF.Tanh, [ab.b], [B8.b], scale=0.5)
            ts("dve", B8.t[:, :], B8.t[:, :], 0.5, 0.5, ALU.mult, ALU.add, [B8.b], [B8.b])
            act(x8.t[:, :], ab.t[0:8, :], AF.Copy, [ab.b], [x8.b])
            ts("dve", x8.t[:, :], x8.t[:, :], colG.t[0:8, 1:2], None, ALU.add, None, [x8.b, colG.b], [x8.b])
            ts("dve", n8.t[:, :], x8.t[:, :], -1.0, None, ALU.mult, None, [x8.b], [n8.b])
            tt("dve", n8.t[:, :], n8.t[:, :], x8.t[:, :], ALU.min, [n8.b, x8.b], [n8.b])
            act(n8.t[:, :], n8.t[:, :], AF.Exp, [n8.b], [n8.b])
            act(n8.t[:, :], n8.t[:, :], AF.Ln, [n8.b, one_t.b], [n8.b], bias=one_t.t[0:8, 0:1], scale=1.0)
            stt("dve", G8.t[:, :], x8.t[:, :], 0.0, n8.t[:, :], ALU.max, ALU.add, [x8.b, n8.b], [G8.b])
            ts("dve", G8.t[:, :], G8.t[:, :], nega.t[0:8, 0:1], None, ALU.mult, None, [G8.b, nega.b], [G8.b])
            kb.op("dve", lambda e: e.tensor_tensor_scan(gc8.t[:, :], m128.t[0:8, :], G8.t[:, :], 0.0, ALU.mult, ALU.add), [m128.b, G8.b], [gc8.b])
            pT_ = pr.next()
            for ck in range(4):
                mm(pT_, pT_.t[:, ck * 8:(ck + 1) * 8], gc8.t[:, ck * 128:(ck + 1) * 128], identf.t[0:8, 0:8], [gc8.b, identf.b])
                mm(pT_, pT_.t[:, 32 + ck * 8:32 + (ck + 1) * 8], B8.t[:, ck * 128:(ck + 1) * 128], identf.t[0:8, 0:8], [B8.b, identf.b])
            cp("dve", gcT.t[:, :, :].rearrange("p c e -> p (c e)"), pT_.t[:, 0:32], [pT_.b], [gcT.b], small=True)
            cp("dve", bT.t[:, :, :].rearrange("p c e -> p (c e)"), pT_.t[:, 32:64], [pT_.b], [bT.b], small=True)

            if KSTOP == 'C':
                break
            for pair in range(2):
                hs = [2 * pair, 2 * pair + 1]
                for n, h in enumerate(hs):
                    W_ = HS[n]
                    gcB = pr.next()
                    if KD >= 1:
                        mm(gcB, gcB.t[:], selb.t[0:8, h * 128:(h + 1) * 128], gc8.t[:, :], [selb.b, gc8.b])
                    W_["gcB"] = gcB
                    if KD >= 2:
                        act(W_["egcB"].t[:], gcB.t[:], AF.Exp, [gcB.b], [W_["egcB"].b])
                    if KD >= 3:
                        act(W_["gcBs"].t[:], gcB.t[:], AF.Copy, [gcB.b], [W_["gcBs"].b])
                        cp("dve", W_["glast"].t[:, :], W_["gcBs"].t[:, 127:TT:128], [W_["gcBs"].b], [W_["glast"].b], small=True)
                    if KD >= 4:
                        act(W_["eglast"].t[:, :], W_["glast"].t[:, :], AF.Exp, [W_["glast"].b], [W_["eglast"].b], small=True)
                    if KD >= 5:
                        act(W_["egcT"].t[:, :], gcT.t[:, :, h], AF.Exp, [gcT.b], [W_["egcT"].b], small=True)
                    if KD >= 6:
                        tt("dve", W_["bgT"].t[:, :], W_["egcT"].t[:, :], bT.t[:, :, 4 + h], ALU.mult, [W_["egcT"].b, bT.b], [W_["bgT"].b], small=True)
                    if KD >= 7:
                        tt("dve", W_["kdcol"].t[:, :], W_["glast"].t[:, :], gcT.t[:, :, h], ALU.subtract, [W_["glast"].b, gcT.b], [W_["kdcol"].b], small=True)
                    if KD >= 8:
                        act(W_["kdcol"].t[:, :], W_["kdcol"].t[:, :], AF.Exp, [W_["kdcol"].b], [W_["kdcol"].b], small=True)
                    if KD >= 9:
                        tt("dve", W_["qdec"].t[:], qs[h].t[:], W_["egcB"].t[:], ALU.mult, [qs[h].b, W_["egcB"].b], [W_["qdec"].b])
                    bB = pr.next()
                    if KD >= 10:
                        mm(bB, bB.t[:], selb.t[0:8, (4 + h) * 128:(5 + h) * 128], B8.t[:, :], [selb.b, B8.b])
                    if KD >= 11:
                        tt("dve", W_["kbT"].t[:], bB.t[:], qs[4 + h].t[:], ALU.mult, [bB.b, qs[4 + h].b], [W_["kbT"].b])
                    if KSTOP == 'D':
                        continue
                    for ck in range(4):
                        c_ = slice(ck * 128, (ck + 1) * 128)
                        ts("dve", W_["E"].t[:, c_], W_["gcBs"].t[:, c_], gcT.t[:, ck, h:h + 1], 0.0, ALU.subtract, ALU.min, [W_["gcBs"].b, gcT.b], [W_["E"].b])
                    act(W_["E"].t[:], W_["E"].t[:], AF.Exp, [W_["E"].b], [W_["E"].b])
                    tt("pool", v4(W_["Es"].t[:, :]), v4(W_["E"].t[:, :]), b4(uts.t[:, :]), ALU.mult, [W_["E"].b, uts.b], [W_["Es"].b])
                    tt("pool", v4(W_["Ei"].t[:, :]), v4(W_["E"].t[:, :]), b4(uti.t[:, :]), ALU.mult, [W_["E"].b, uti.b], [W_["Ei"].b])
                    kk = pr.next()
                    for ck in range(4):
                        c_ = slice(ck * 128, (ck + 1) * 128)
                        mm(kk, kk.t[:, c_], knb[h].t[:, c_], W_["kbT"].t[:, c_], [knb[h].b, W_["kbT"].b])
                    tt("dve", W_["U"].t[:], kk.t[:], W_["Es"].t[:], ALU.mult, [kk.b, W_["Es"].b], [W_["U"].b])
                    qk = pr.next()
                    for ck in range(4):
                        c_ = slice(ck * 128, (ck + 1) * 128)
                        mm(qk, qk.t[:, c_], knb[h].t[:, c_], qnb[h].t[:, c_], [knb[h].b, qnb[h].b])
                    tt("dve", W_["intraT"].t[:], qk.t[:], W_["Ei"].t[:], ALU.mult, [qk.b, W_["Ei"].b], [W_["intraT"].b])
                    if KSTOP == 'E':
                        continue
                    lp = pr.next()
                    lpv = lp.t[:].bitcast(BF16)
                    for ck in range(4):
                        c_ = slice(ck * 128, (ck + 1) * 128)
                        kb.op("pe", lambda e, lpv=lpv, W_=W_, c_=c_: e.transpose(lpv[:, c_], W_["U"].t[:, c_], ident.t[:, :]), [W_["U"].b, ident.b], [lp.b])
                    act(W_["L"].t[:], lpv[:, 0:TT], AF.Copy, [lp.b], [W_["L"].b])
                    tt("pool", v4(W_["P"].t[:, :]), b4(ident.t[:, :]), v4(W_["U"].t[:, :]), ALU.subtract, [W_["U"].b, ident.b], [W_["P"].b])
                if KSTOP in ('D', 'E', 'F'):
                    break
                cur = [("U", "L") for _ in hs]
                nxt = [("U2", "L2") for _ in hs]
                for s_ in range(1, 7):
                    for n, h in enumerate(hs):
                        W_ = HS[n]
                        Uc, Lc = W_[cur[n][0]], W_[cur[n][1]]
                        Un, Ln = W_[nxt[n][0]], W_[nxt[n][1]]
                        if s_ < 6:
                            pu = pr.next()
                            for ck in range(4):
                                c_ = slice(ck * 128, (ck + 1) * 128)
                                mm(pu, pu.t[:, c_], Lc.t[:, c_], Uc.t[:, c_], [Lc.b, Uc.b])
                            act(Un.t[:], pu.t[:], AF.Copy, [pu.b], [Un.b])
                        pl = pr.next()
                        for ck in range(4):
                            c_ = slice(ck * 128, (ck + 1) * 128)
                            mm(pl, pl.t[:, c_], Uc.t[:, c_], Lc.t[:, c_], [Lc.b, Uc.b])
                        cp("dve", Ln.t[:], pl.t[:], [pl.b], [Ln.b])
                        pp_ = pr.next()
                        for ck in range(4):
                            c_ = slice(ck * 128, (ck + 1) * 128)
                            mm(pp_, pp_.t[:, c_], Ln.t[:, c_], W_["P"].t[:, c_], [W_["P"].b, Ln.b])
                        tt("dve", W_["P"].t[:], pp_.t[:], W_["P"].t[:], ALU.add, [pp_.b, W_["P"].b], [W_["P"].b])
                    cur, nxt = nxt, cur
                if KSTOP == 'G':
                    break
                for n, h in enumerate(hs):
                    W_ = HS[n]
                    kp = pr.next()
                    kpv = kp.t[:].bitcast(BF16)
                    vp = pr.next()
                    vpv = vp.t[:].bitcast(BF16)
                    for ck in range(4):
                        c_ = slice(ck * 128, (ck + 1) * 128)
                        kb.op("pe", lambda e, kpv=kpv, h=h, c_=c_: e.transpose(kpv[:, c_], knb[h].t[:, c_], ident.t[:, :]), [knb[h].b, ident.b], [kp.b])
                        kb.op("pe", lambda e, vpv=vpv, h=h, c_=c_: e.transpose(vpv[:, c_], vT[h].t[:, c_], ident.t[:, :]), [vT[h].b, ident.b], [vp.b])
                    for ck in range(4):
                        c_ = slice(ck * 128, (ck + 1) * 128)
                        act(W_["kbg"].t[:, c_], kpv[:, c_], AF.Copy, [kp.b, W_["bgT"].b], [W_["kbg"].b], scale=W_["bgT"].t[:, ck:ck + 1])
                        act(W_["kdec"].t[:, c_], kpv[:, c_], AF.Copy, [kp.b, W_["kdcol"].b], [W_["kdec"].b], scale=W_["kdcol"].t[:, ck:ck + 1])
                        ts("dve", W_["vb"].t[:, c_], vpv[:, c_], bT.t[:, ck, 4 + h:5 + h], None, ALU.mult, None, [vp.b, bT.b], [W_["vb"].b])
                    wp = pr.next()
                    for ck in range(4):
                        c_ = slice(ck * 128, (ck + 1) * 128)
                        mm(wp, wp.t[:, c_], W_["kbg"].t[:, c_], W_["P"].t[:, c_], [W_["kbg"].b, W_["P"].b])
                    act(W_["wTn"].t[:], wp.t[:], AF.Copy, [wp.b], [W_["wTn"].b], scale=-1.0)
                if KSTOP == 'H':
                    break
                ops_ = [ps[6], ps[7]]
                for ck in range(4):
                    c_ = slice(ck * 128, (ck + 1) * 128)
                    vps = []
                    for n, h in enumerate(hs):
                        W_ = HS[n]
                        vn_ps = pr.next()
                        mm(vn_ps, vn_ps.t[:, 0:128], W_["P"].t[:, c_], W_["vb"].t[:, c_], [W_["P"].b, W_["vb"].b], start=True, stop=False)
                        mm(vn_ps, vn_ps.t[:, 0:128], W_["wTn"].t[:, c_], Sgb[h].t[:, :], [W_["wTn"].b, Sgb[h].b], start=False, stop=True)
                        vps.append(vn_ps)
                    for n, h in enumerate(hs):
                        W_ = HS[n]
                        act(W_["vn"].t[:, c_], vps[n].t[:, 0:128], AF.Copy, [vps[n].b], [W_["vn"].b])
                    dps = []
                    for n, h in enumerate(hs):
                        W_ = HS[n]
                        mm(ops_[n], ops_[n].t[:, c_], Sgb[h].t[:, :], W_["qdec"].t[:, c_], [Sgb[h].b, W_["qdec"].b], start=True, stop=False)
                        mm(ops_[n], ops_[n].t[:, c_], W_["vn"].t[:, c_], W_["intraT"].t[:, c_], [W_["vn"].b, W_["intraT"].b], start=False, stop=True)
                        d_ps = pr.next()
                        mm(d_ps, d_ps.t[:, 0:128], W_["kdec"].t[:, c_], W_["vn"].t[:, c_], [W_["kdec"].b, W_["vn"].b])
                        dps.append(d_ps)
                    for n, h in enumerate(hs):
                        W_ = HS[n]
                        stt("dve", Sg.t[:, h, :], Sg.t[:, h, :], W_["eglast"].t[:, ck:ck + 1], dps[n].t[:, 0:128], ALU.mult, ALU.add,
                            [Sg.b, W_["eglast"].b, dps[n].b], [Sg.b])
                        cp("dve", Sgb[h].t[:, :], Sg.t[:, h, :], [Sg.b], [Sgb[h].b])
                if KSTOP == 'I':
                    break
                for n, h in enumerate(hs):
                    W_ = HS[n]
                    o_ps = ops_[n]
                    act(W_["osb"].t[:], o_ps.t[:], AF.Copy, [o_ps.b], [W_["osb"].b])
                    act(W_["sq"].t[:], o_ps.t[:], AF.Square, [o_ps.b], [W_["sq"].b])
                    ss = pr.next()
                    mm(ss, ss.t[:], ones.t[:, :], W_["sq"].t[:, :], [ones.b, W_["sq"].b])
                    act(W_["rstd"].t[:], ss.t[:], AF.Ln, [ss.b, eps_t.b], [W_["rstd"].b], bias=eps_t.t[:, 0:1], scale=1.0 / 128)
                    act(W_["rstd"].t[:], W_["rstd"].t[:], AF.Exp, [W_["rstd"].b], [W_["rstd"].b], scale=-0.5)
                    stt("dve", W_["osb"].t[:], W_["osb"].t[:], colC.t[:, 1:2], W_["rstd"].t[:], ALU.mult, ALU.mult, [W_["osb"].b, colC.b, W_["rstd"].b], [W_["osb"].b])
                    y = yr.next()
                    tt("pool", y.t[:], W_["osb"].t[:], szb[h].t[:], ALU.mult, [W_["osb"].b, szb[h].b], [y.b])
                    store(Yd[1, 128 * h:128 * (h + 1), sl], y, y.t[:, :])

    def p3a(l, sb, psums):
        ps = psums()
        pr = Rot(ps)
        stage = Rot([sb([128, 1024], F32) for _ in range(3)])
        sdsem = {id(s): kb.new_dsem() for s in stage.items}
        engs = Rot(["pool", "dve", "act"])
        Wg = sb([128, 8, 3072], BF16)
        for part in range(3):
            load_w_bf16(Wg, lambda kc, part=part: Wg.t[:, kc, part * 1024:(part + 1) * 1024],
                        lambda kc, part=part: w_in[l, kc * 128:(kc + 1) * 128, O_G + part * 1024:O_G + (part + 1) * 1024], 8, 1024, stage, sdsem, engs)
        Wbr = sb([128, 12, D], BF16)
        load_w_bf16(Wbr, lambda kc: Wbr.t[:, kc, :], lambda kc: w_br[l, kc // 4, (kc % 4) * 128:(kc % 4 + 1) * 128, :], 12, D, stage, sdsem, engs)
        xb = [sb([128, 8, TT], BF16) for _ in range(2)]
        yb = [sb([128, 12, TT], BF16) for _ in range(2)]
        ld = [kb.new_dsem() for _ in range(2)]
        mg = [sb([128, 8, TT], BF16) for _ in range(2)]
        mg_d = [kb.new_dsem() for _ in range(2)]
        mf = Rot([sb([128, TT], F32) for _ in range(2)])
        gsb = Rot([sb([128, TT], F32) for _ in range(3)])
        tsb = Rot([sb([128, TT], F32) for _ in range(3)])

        def issue_loads(t):
            i = t % 2
            sl = slice(t * TT, (t + 1) * TT)
            kb.dma(xb[i].t[:], XTb[:, sl].rearrange("(c p) t -> p c t", p=128), [], [xb[i].b], ld[i])
            for n in range(3):
                kb.dma(yb[i].t[:, 4 * n:4 * n + 4, :], Yd[n, :, sl].rearrange("(c p) t -> p c t", p=128), [], [yb[i].b], ld[i])

        issue_loads(0)
        for t in range(NT):
            i = t % 2
            sl = slice(t * TT, (t + 1) * TT)
            if t + 1 < NT:
                issue_loads(t + 1)
            for dc in range(8):
                m_ = mf.next()
                for n in range(3):
                    gp = pr.next()
                    for kc in range(8):
                        mm(gp, gp.t[:], Wg.t[:, kc, n * D + dc * 128:n * D + (dc + 1) * 128], xb[i].t[:, kc, :], [Wg.b, xb[i].b], start=(kc == 0), stop=(kc == 7))
                    g = gsb.next()
                    act(g.t[:], gp.t[:], AF.Sigmoid, [gp.b], [g.b])
                    pp_ = pr.next()
                    for kc in range(4):
                        mm(pp_, pp_.t[:], Wbr.t[:, 4 * n + kc, dc * 128:(dc + 1) * 128], yb[i].t[:, 4 * n + kc, :], [Wbr.b, yb[i].b], start=(kc == 0), stop=(kc == 3))
                    if n == 0:
                        tt("dve", m_.t[:], pp_.t[:], g.t[:], ALU.mult, [pp_.b, g.b], [m_.b])
                    else:
                        tm = tsb.next()
                        tt("dve", tm.t[:], pp_.t[:], g.t[:], ALU.mult, [pp_.b, g.b], [tm.b])
                        if n == 1:
                            tt("pool", m_.t[:], m_.t[:], tm.t[:], ALU.add, [m_.b, tm.b], [m_.b])
                        else:
                            tt("pool", mg[i].t[:, dc, :], m_.t[:], tm.t[:], ALU.add, [m_.b, tm.b], [mg[i].b])
            kb.dma(MGd[:, sl].rearrange("(c p) t -> p c t", p=128), mg[i].t[:], [mg[i].b], [Buf()], mg_d[i])

    def p3b(l, sb, psums):
        src_x = xT_in if l == 0 else XT
        dst_x = outT if l == L - 1 else XT
        ps = psums()
        pr = Rot(ps[0:6])
        cs = load_consts(sb, [("one", C_ONE, C_ONE + 128, BF16, 128), ("onef", C_ONE, C_ONE + 128, F32, 128)])
        ones = cs["one"]
        onef = cs["onef"]
        stage = Rot([sb([128, 1024], F32) for _ in range(3)])
        sdsem = {id(s): kb.new_dsem() for s in stage.items}
        engs = Rot(["pool", "dve", "act"])
        Wo = sb([128, 8, D], BF16)
        load_w_bf16(Wo, lambda kc: Wo.t[:, kc, :], lambda kc: w_out[l, kc * 128:(kc + 1) * 128, :], 8, D, stage, sdsem, engs)
        Wpg = sb([128, 8, D], BF16)
        load_w_bf16(Wpg, lambda kc: Wpg.t[:, kc, :], lambda kc: ple_gate[l, kc * 128:(kc + 1) * 128, :], 8, D, stage, sdsem, engs)
        Wpp = sb([128, 2, D], BF16)
        load_w_bf16(Wpp, lambda kc: Wpp.t[:, kc, :], lambda kc: ple_proj[l, kc * 128:(kc + 1) * 128, :], 2, D, stage, sdsem, engs)
        colB = sb([128, 64], F32)
        kb.dma(colB.t[:], colsB[l], [], [colB.b], kb.new_dsem())
        eps_t = sb([128, 1], F32)
        kb.op("pool", lambda e: e.memset(eps_t.t[:], EPS), [], [eps_t.b])

        xf = [sb([128, 8, TT], F32) for _ in range(2)]
        mg = [sb([128, 8, TT], BF16) for _ in range(2)]
        pf = [sb([128, 2, TT], F32) for _ in range(2)]
        ld = [kb.new_dsem() for _ in range(2)]
        pb = sb([128, 2, TT], BF16)
        r = sb([128, 8, TT], F32)
        rb = sb([128, 8, TT], BF16)
        r2b = sb([128, 8, TT], BF16)
        sq = sb([128, 8, TT], BF16)
        xob = [sb([128, 8, TT], BF16) for _ in range(2)]
        gsb = Rot([sb([128, TT], F32) for _ in range(3)])
        tsb = Rot([sb([128, TT], F32) for _ in range(3)])
        mean = sb([128, TT], F32)
        msq = sb([128, TT], F32)
        rstd = sb([128, TT], F32)
        xo_d = [kb.new_dsem() for _ in range(2)]
        xob_d = [kb.new_dsem() for _ in range(2)]

        def issue_loads(t):
            i = t % 2
            sl = slice(t * TT, (t + 1) * TT)
            kb.dma(xf[i].t[:], src_x[:, sl].rearrange("(c p) t -> p c t", p=128), [], [xf[i].b], ld[i])
            kb.dma(mg[i].t[:], MGd[:, sl].rearrange("(c p) t -> p c t", p=128), [], [mg[i].b], ld[i])
            kb.dma(pf[i].t[:], pT_in[l, :, sl].rearrange("(c p) t -> p c t", p=128), [], [pf[i].b], ld[i])

        issue_loads(0)
        for t in range(NT):
            i = t % 2
            sl = slice(t * TT, (t + 1) * TT)
            if t + 1 < NT:
                issue_loads(t + 1)
            for kc in range(2):
                cp("pool", pb.t[:, kc, :], pf[i].t[:, kc, :], [pf[i].b], [pb.b])
            for dc in range(8):
                rp = pr.next()
                for kc in range(8):
                    mm(rp, rp.t[:], Wo.t[:, kc, dc * 128:(dc + 1) * 128], mg[i].t[:, kc, :], [Wo.b, mg[i].b], start=(kc == 0), stop=(kc == 7))
                stt("dve", r.t[:, dc, :], xf[i].t[:, dc, :], ALPHA, rp.t[:], ALU.mult, ALU.add, [xf[i].b, rp.b], [r.b])
                act(rb.t[:, dc, :], r.t[:, dc, :], AF.Copy, [r.b], [rb.b])
            s1 = ps[6]
            s2 = ps[7]
            for dc in range(8):
                up = pr.next()
                for kc in range(8):
                    mm(up, up.t[:], Wpg.t[:, kc, dc * 128:(dc + 1) * 128], rb.t[:, kc, :], [Wpg.b, rb.b], start=(kc == 0), stop=(kc == 7))
                g = gsb.next()
                act(g.t[:], up.t[:], AF.Tanh, [up.b], [g.b], scale=0.5)
                pq = pr.next()
                for kc in range(2):
                    mm(pq, pq.t[:], Wpp.t[:, kc, dc * 128:(dc + 1) * 128], pb.t[:, kc, :], [Wpp.b, pb.b], start=(kc == 0), stop=(kc == 1))
                tm = tsb.next()
                stt("dve", tm.t[:], g.t[:], 1.0, pq.t[:], ALU.add, ALU.mult, [pq.b, g.b], [tm.b])
                stt("dve", r.t[:, dc, :], tm.t[:], 0.5, r.t[:, dc, :], ALU.mult, ALU.add, [r.b, tm.b], [r.b])
                act(sq.t[:, dc, :], r.t[:, dc, :], AF.Square, [r.b], [sq.b])
                cp("dve", r2b.t[:, dc, :], r.t[:, dc, :], [r.b], [r2b.b])
            for dc in range(8):
                mm(s1, s1.t[:], ones.t[:, :], r2b.t[:, dc, :], [ones.b, r2b.b], start=(dc == 0), stop=(dc == 7))
            for dc in range(8):
                mm(s2, s2.t[:], ones.t[:, :], sq.t[:, dc, :], [ones.b, sq.b], start=(dc == 0), stop=(dc == 7))
            act(mean.t[:], s1.t[:], AF.Copy, [s1.b], [mean.b], scale=1.0 / D)
            tt("pool", msq.t[:], mean.t[:], mean.t[:], ALU.mult, [mean.b], [msq.b])
            stt("dve", msq.t[:], s2.t[:], 1.0 / D, msq.t[:], ALU.mult, ALU.subtract, [s2.b, msq.b], [msq.b])
            act(rstd.t[:], msq.t[:], AF.Ln, [msq.b, eps_t.b], [rstd.b], bias=eps_t.t[:, 0:1], scale=1.0)
            act(rstd.t[:], rstd.t[:], AF.Exp, [rstd.b], [rstd.b], scale=-0.5)
            o = xf[i]
            for dc in range(8):
                e1 = ("pool", "dve")[dc % 2]
                tt(e1, r.t[:, dc, :], r.t[:, dc, :], mean.t[:], ALU.subtract, [r.b, mean.b], [r.b])
                tt(e1, r.t[:, dc, :], r.t[:, dc, :], rstd.t[:], ALU.mult, [r.b, rstd.b], [r.b])
                act(o.t[:, dc, :], r.t[:, dc, :], AF.Identity, [r.b, colB.b], [o.b], scale=colB.t[:, dc:dc + 1], bias=colB.t[:, 8 + dc:9 + dc])
                if l < L - 1:
                    cp("pool", xob[i].t[:, dc, :], o.t[:, dc, :], [o.b], [xob[i].b])
            kb.dma(dst_x[:, sl].rearrange("(c p) t -> p c t", p=128), o.t[:], [o.b], [Buf()], xo_d[i])
            if l < L - 1:
                kb.dma(XTb[:, sl].rearrange("(c p) t -> p c t", p=128), xob[i].t[:], [xob[i].b], [Buf()], xob_d[i])

    _cm = nc.allow_non_contiguous_dma(reason="small strided scratch/const transfers")
    _cm.__enter__()
    phase(p0, "p0")
    for l in range(L):
        phase(lambda sb, psums, l=l: p1a(l, sb, psums), "p1a")
        phase(lambda sb, psums, l=l: p2(l, sb, psums), "p2")
        if with_b:
            phase(lambda sb, psums, l=l: p1b(l, sb, psums), "p1b")
        else:
            phase(lambda sb, psums: pzero(1, sb, psums))
        if with_c:
            phase(lambda sb, psums, l=l: p1c(l, sb, psums), "p1c")
        else:
            phase(lambda sb, psums: pzero(2, sb, psums))
        if dbg and l == L - 1:
            def pdbg(sb, psums):
                tl = sb([128, 12, T], BF16)
                d = kb.new_dsem()
                for n in range(3):
                    kb.dma(tl.t[:, 4 * n:4 * n + 4, :], Yd[n].rearrange("(c p) t -> p c t", p=128), [], [tl.b], d)
                d2 = kb.new_dsem()
                for n in range(3):
                    kb.dma(dbg_y[n].rearrange("(c p) t -> p c t", p=128), tl.t[:, 4 * n:4 * n + 4, :], [tl.b], [Buf()], d2)
            phase(pdbg)
        phase(lambda sb, psums, l=l: p3a(l, sb, psums), "p3a")
        phase(lambda sb, psums, l=l: p3b(l, sb, psums), "p3b")
    _cm.__exit__(None, None, None)
    return nc, kb


def prep_weights(inp, L):
    f = np.float32
    w_uq = np.asarray(inp["a_w_uq"], f)[:L].reshape(L, 256, 8, 96)
    w_uqn = np.ascontiguousarray(w_uq[:, :, :, 0:64].reshape(L, 256, 512))
    w_uqr = np.ascontiguousarray(w_uq[:, :, :, 64:96].reshape(L, 256, 256))
    w_ukv = np.asarray(inp["a_w_ukv"], f)[:L].reshape(L, 128, 8, 128)
    w_uk = np.ascontiguousarray(w_ukv[:, :, :, 0:64].reshape(L, 128, 512))
    w_uv = np.ascontiguousarray(w_ukv[:, :, :, 64:128].reshape(L, 128, 512))
    colsA = np.zeros((L, 128, 16), f)
    colsA[:, :, 0:2] = np.asarray(inp["a_q_norm"], f)[:L].reshape(L, 2, 128).transpose(0, 2, 1)
    colsA[:, :, 2] = np.asarray(inp["a_kv_norm"], f)[:L]
    colsB = np.zeros((L, 128, 64), f)
    colsB[:, :, 0:8] = np.asarray(inp["ln_g"], f)[:L].reshape(L, 8, 128).transpose(0, 2, 1)
    colsB[:, :, 8:16] = np.asarray(inp["ln_b"], f)[:L].reshape(L, 8, 128).transpose(0, 2, 1)
    colsC = np.zeros((L, 128, 32), f)
    colsC[:, :, 0] = np.asarray(inp["c_norm"], f)[:L]
    colsC[:, :, 1] = np.asarray(inp["b_norm"], f)[:L]
    lg = np.asarray(inp["c_lb_logits"], f).reshape(4, 4, 128)
    colsC[:, :, 8:24] = lg.transpose(2, 1, 0).reshape(128, 16)[None]
    colsW = np.ascontiguousarray(np.asarray(inp["b_conv"], f)[:L, :, 0, :].reshape(L, 4, 12, 128).transpose(0, 3, 2, 1))
    colsG = np.zeros((L, 128, 8), f)
    colsG[:, 0:4, 0] = np.asarray(inp["b_a_log"], f)[:L]
    colsG[:, 0:4, 1] = np.asarray(inp["b_dt_bias"], f)[:L]
    return dict(
        colsW=colsW, colsG=colsG, colsC=colsC, w_in=np.ascontiguousarray(np.asarray(inp["w_in"], f)[:L]),
        w_uqn=w_uqn, w_uqr=w_uqr, w_uk=w_uk, w_uv=w_uv, colsA=colsA, colsB=colsB,
        w_br=np.ascontiguousarray(np.asarray(inp["w_branch"], f)[:L]),
        w_out=np.ascontiguousarray(np.asarray(inp["w_out"], f)[:L]),
        ple_proj=np.ascontiguousarray(np.asarray(inp["ple_proj"], f)[:L]),
        ple_gate=np.ascontiguousarray(np.asarray(inp["ple_gate"], f)[:L]),
        cst=make_consts(),
    )


def run(inp, T, L, B, trace=False, **bkw):
    nc, kb = build(T, L, **bkw)
    shared = prep_weights(inp, L)
    x = np.asarray(inp["x"], np.float32)
    p = np.asarray(inp["p"], np.float32)
    pos = np.asarray(inp["positions"], np.int32)
    in_maps = []
    for b in range(B):
        m = dict(shared)
        m["xT"] = np.ascontiguousarray(x[b, :T].T)
        m["pT"] = np.ascontiguousarray(p[:L, b, :T].transpose(0, 2, 1))
        m["pos"] = np.ascontiguousarray(pos[b:b + 1, :T])
        in_maps.append(m)
    res = run_bass_kernel_spmd(nc, in_maps, core_ids=list(range(B)), **({"trace": True} if trace else {}))
    return res


def kernel(**inputs):
    res = run(inputs, 8192, 4, 4)
    out = np.stack([np.ascontiguousarray(r["outT"].T) for r in res.results], axis=0)
    return out.astype(np.float32)
```

```python
import math
from contextlib import ExitStack
import numpy as np
import concourse.bass as bass
import concourse.mybir as mybir
from concourse.bass_utils import run_bass_kernel_spmd

F32 = mybir.dt.float32
BF16 = mybir.dt.bfloat16
I32 = mybir.dt.int32
AF = mybir.ActivationFunctionType
ALU = mybir.AluOpType

D = 1024
NL = 4
PLE = 256
INW = 8104
A_SCALE = 96 ** -0.5
EPS = 1e-6
ALPHA = (2.0 * 4) ** 0.25
O_CQ, O_CKV, O_KR, O_ZA, O_QKV, O_BA, O_BB, O_ZB, O_HQ, O_HF, O_HI, O_ZC, O_G = (
    0, 256, 384, 416, 928, 2464, 2468, 2472, 2984, 3496, 4008, 4520, 5032)
NP1 = 5032
TT = 512


class Buf:
    __slots__ = ("last_w", "readers")

    def __init__(self):
        self.last_w = None
        self.readers = []


class Op:
    __slots__ = ("eng", "fn", "waits", "signal", "dsem", "count", "small", "cost", "lat", "tab")

    def __init__(self, eng, fn, dsem=None, small=False, cost=300.0, lat=0.0, tab=None):
        self.eng = eng
        self.fn = fn
        self.waits = []
        self.signal = False
        self.dsem = dsem
        self.count = None
        self.small = small
        self.cost = cost
        self.lat = lat
        self.tab = tab


ENGS = ("pe", "act", "dve", "pool", "sp")
TABS = {AF.Exp: "A", AF.Ln: "A", AF.Tanh: "A", AF.Sigmoid: "B", AF.Silu: "C", AF.Sin: "D"}


class KB:
    def __init__(self, nc):
        self.nc = nc
        self.sems = {}
        self.counts = {}
        self.seen = {e: {} for e in ENGS}
        self.ops = []
        self.ndsem = 0
        self.ninst = 0
        self.reorder = True
        self.window = 0
        self.sim_total = 0.0

    def new_dsem(self):
        self.ndsem += 1
        return ("d", self.ndsem - 1)

    def begin(self):
        self.ops = []
        self.ndsem = 0
        self.ld_sem = {}
        self.st_sem = {}
        self.nd2 = 0
        self._keep = []

    def op(self, eng, fn, reads=(), writes=(), dsem=None, small=False, cost=300.0, lat=0.0, tab=None):
        idx = len(self.ops)
        o = Op(eng, fn, dsem, small, cost, lat, tab)
        deps = set()
        for b in reads:
            if b.last_w is not None:
                deps.add(b.last_w)
        for b in writes:
            if b.last_w is not None:
                deps.add(b.last_w)
            deps.update(b.readers)
        for b in reads:
            b.readers.append(idx)
        for b in writes:
            b.last_w = idx
            b.readers = []
        deps.discard(idx)
        o.waits = sorted(deps)
        self.ops.append(o)
        return idx

    def dma(self, out, in_, reads, writes, dsem=None, q="sp"):
        if len(reads) == 0:
            kbuf, tab = writes[0], self.ld_sem
        else:
            kbuf, tab = reads[0], self.st_sem
        if id(kbuf) not in tab:
            tab[id(kbuf)] = ("d", self.nd2)
            self.nd2 += 1
            self._keep.append(kbuf)
        nbytes = 1
        for d in out.shape:
            nbytes *= d
        return self.op(q, lambda e: e.dma_start(out=out, in_=in_), reads, writes, dsem=tab[id(kbuf)], cost=60.0, lat=2000.0 + nbytes * 4 / 100.0)

    def _schedule(self, ops):
        import heapq
        n = len(ops)
        succ = [[] for _ in range(n)]
        ndep = [0] * n
        for i, o in enumerate(ops):
            ndep[i] = len(o.waits)
            for d in o.waits:
                succ[d].append(i)
        ready_t = [0.0] * n
        fin = [0.0] * n
        pend = {e: [] for e in ENGS}
        avail = {e: [] for e in ENGS}
        clock = {e: 0.0 for e in ENGS}
        for i, o in enumerate(ops):
            if ndep[i] == 0:
                heapq.heappush(pend[o.eng], (0.0, i))
        order = []
        done = 0
        cur_tab = None
        act_av = {}
        SWITCH = 1400.0
        while done < n:
            best = None
            for e in ENGS:
                pe_, av = pend[e], avail[e]
                while pe_ and pe_[0][0] <= clock[e]:
                    i_ = heapq.heappop(pe_)[1]
                    if e == "act":
                        heapq.heappush(act_av.setdefault(ops[i_].tab, []), i_)
                    else:
                        heapq.heappush(av, i_)
                if e == "act":
                    c1 = [h_[0] for t_, h_ in act_av.items() if h_ and (t_ is None or t_ == cur_tab)]
                    c2 = [h_[0] for t_, h_ in act_av.items() if h_]
                    if c1:
                        cand = (clock[e], min(c1), e, True)
                    elif c2:
                        cand = (clock[e] + SWITCH, min(c2), e, True)
                    elif pe_:
                        cand = (pe_[0][0], pe_[0][1], e, False)
                    else:
                        continue
                elif av:
                    cand = (clock[e], av[0], e, True)
                elif pe_:
                    cand = (pe_[0][0], pe_[0][1], e, False)
                else:
                    continue
                if best is None or cand[:2] < best[:2]:
                    best = cand
            st, i, e, from_av = best
            if e == "act":
                if from_av:
                    heapq.heappop(act_av[ops[i].tab])
                else:
                    heapq.heappop(pend[e])
                    if ops[i].tab is not None and ops[i].tab != cur_tab:
                        st += SWITCH
                if ops[i].tab is not None:
                    cur_tab = ops[i].tab
            elif from_av:
                heapq.heappop(avail[e])
            else:
                heapq.heappop(pend[e])
            o = ops[i]
            clock[e] = st + o.cost
            fin[i] = st + o.cost + o.lat
            order.append(i)
            done += 1
            for j in succ[i]:
                ndep[j] -= 1
                lat_x = 0.0 if ops[j].eng == e and o.dsem is None else 180.0
                if fin[i] + lat_x > ready_t[j]:
                    ready_t[j] = fin[i] + lat_x
                if ndep[j] == 0:
                    heapq.heappush(pend[ops[j].eng], (ready_t[j], j))
        self.sim_time = max(fin) if fin else 0.0
        pos = {old: new for new, old in enumerate(order)}
        out = []
        for old in order:
            o = ops[old]
            o.waits = sorted(pos[d] for d in o.waits)
            out.append(o)
        return out

    def _sem(self, k):
        if k not in self.sems:
            nm = "s_" + (k if isinstance(k, str) else "d%d" % k[1])
            self.sems[k] = self.nc.alloc_semaphore(nm)
        return self.sems[k]

    def end(self):
        nc = self.nc
        ops = self.ops
        if self.reorder and len(ops) > 1:
            ops = self._schedule(ops)

        def key(o):
            return o.dsem if o.dsem is not None else o.eng

        for o in ops:
            kept = []
            for d in o.waits:
                p = ops[d]
                if p.dsem is None and p.eng == o.eng and not (p.small or o.small):
                    continue
                p.signal = True
                kept.append(d)
            o.waits = kept
        dkeys = set()
        for o in ops:
            if o.dsem is not None:
                o.signal = True
                dkeys.add(o.dsem)
            if o.signal:
                k = key(o)
                self.counts[k] = self.counts.get(k, 0) + (16 if o.dsem is not None else 1)
                o.count = self.counts[k]
        plan = {e: [] for e in ENGS}
        for o in ops:
            need = {}
            for d in o.waits:
                p = ops[d]
                k = key(p)
                if p.count > self.seen[o.eng].get(k, 0):
                    need[k] = max(need.get(k, 0), p.count)
            for k, v in need.items():
                self.seen[o.eng][k] = v
            plan[o.eng].append((o, sorted(need.items(), key=str)))
        final = [(k, self.counts[k]) for k in sorted(dkeys, key=str)]
        for k, v in final:
            self.seen["sp"][k] = max(self.seen["sp"].get(k, 0), v)
        for o in ops:
            if o.signal:
                self._sem(key(o))
        sems = self.sems
        self.ninst += len(ops)

        with nc.Block() as block:
            def run(engname):
                def body(e):
                    for o, waits in plan[engname]:
                        for k, v in waits:
                            e.wait_ge(sems[k], v)
                        ins = o.fn(e)
                        if o.signal:
                            ins.then_inc(sems[key(o)], 16 if o.dsem is not None else 1)
                    if engname == "sp":
                        for k, v in final:
                            e.wait_ge(sems[k], v)
                return body

            block.tensor(run("pe"))
            block.scalar(run("act"))
            block.vector(run("dve"))
            block.gpsimd(run("pool"))
            block.sync(run("sp"))
        nc.all_engine_barrier()
        self.ops = []


class Tile:
    def __init__(self, t, nb=1):
        self.t = t
        self.b = Buf()
        self.bs = [Buf() for _ in range(nb)] if nb > 1 else [self.b]


C_ID, C_UTI, C_UTS, C_CM, C_M32, C_M128, C_C32, C_SELB, C_INV, C_ONE, C_END = (
    0, 128, 256, 384, 384 + 2048, 384 + 2560, 384 + 3072, 384 + 3104, 384 + 3104 + 1024,
    384 + 3104 + 1025, 384 + 3104 + 1025 + 128)


def make_consts():
    c = np.zeros((128, C_END), np.float32)
    i = np.arange(128)
    c[:, C_ID:C_ID + 128] = np.eye(128)
    c[:, C_UTI:C_UTI + 128] = (i[:, None] <= i[None, :])
    c[:, C_UTS:C_UTS + 128] = (i[:, None] < i[None, :])
    q = np.arange(512)
    for m in range(4):
        c[:, C_CM + 512 * m:C_CM + 512 * (m + 1)] = (q[None, :] - i[:, None] - 128 * m >= 0)
    c[:, C_M32:C_M32 + 512] = (q % 32 != 0)[None, :]
    c[:, C_M128:C_M128 + 512] = (q % 128 != 0)[None, :]
    j = np.arange(32)
    c[0:32, C_C32:C_C32 + 32] = (j[None, :] >= j[:, None])
    for r in range(8):
        c[r, C_SELB + 128 * r:C_SELB + 128 * (r + 1)] = 1.0
    c[:, C_INV] = (10000.0 ** (-(np.arange(0, 32, 2, dtype=np.float32)) / 32.0)).astype(np.float32)[i % 16]
    c[:, C_ONE:C_ONE + 128] = 1.0
    return c


def build(T, L, with_b=True, with_c=True, dbg=False, only=None):
    NT = T // TT
    NB = T // 128
    nc = bass.Bass("TRN2", target_bir_lowering=False)
    kb = KB(nc)

    def dram_in(name, shape, dt=F32):
        return nc.dram_tensor(name, list(shape), dt, kind="ExternalInput").ap()

    def dram_sc(name, shape, dt):
        return nc.dram_tensor(name, list(shape), dt, kind="Internal").ap()

    xT_in = dram_in("xT", [D, T])
    pT_in = dram_in("pT", [L, PLE, T])
    pos_in = dram_in("pos", [1, T], I32)
    cst_in = dram_in("cst", [128, C_END])
    w_in = dram_in("w_in", [L, D, INW])
    w_uqn = dram_in("w_uqn", [L, 256, 512])
    w_uqr = dram_in("w_uqr", [L, 256, 256])
    w_uk = dram_in("w_uk", [L, 128, 512])
    w_uv = dram_in("w_uv", [L, 128, 512])
    colsA = dram_in("colsA", [L, 128, 16])
    colsB = dram_in("colsB", [L, 128, 64])
    colsC = dram_in("colsC", [L, 128, 32])
    colsW = dram_in("colsW", [L, 128, 12, 4])
    colsG = dram_in("colsG", [L, 128, 8])
    w_br = dram_in("w_br", [L, 3, 512, D])
    w_out = dram_in("w_out", [L, D, D])
    ple_proj = dram_in("ple_proj", [L, PLE, D])
    ple_gate = dram_in("ple_gate", [L, D, D])
    outT = nc.dram_tensor("outT", [D, T], F32, kind="ExternalOutput").ap()

    XT = dram_sc("XT", [D, T], F32)
    COSd = dram_sc("COSd", [128, T], F32)
    SINd = dram_sc("SINd", [128, T], F32)
    Qd = dram_sc("Qd", [8, 96, T], BF16)
    Kd = dram_sc("Kd", [8, 64, T], BF16)
    KRd = dram_sc("KRd", [32, T], BF16)
    Vd = dram_sc("Vd", [8, 128, NB, 64], BF16)
    XTb = dram_sc("XTb", [D, T], BF16)
    MGd = dram_sc("MGd", [D, T], BF16)
    SZAd = dram_sc("SZAd", [512, T], F32)
    Yd = dram_sc("Yd", [3, 512, T], BF16)
    if dbg:
        dbg_y = nc.dram_tensor("dbg_y", [3, 512, T], BF16, kind="ExternalOutput").ap()
        dbg_s = nc.dram_tensor("dbg_s", [128, 8], F32, kind="ExternalOutput").ap()

    pid = [0]

    def phase(fn, name=None):
        if only is not None and name not in only:
            return
        with ExitStack() as st:
            kb.begin()
            cnt = [0]
            pid[0] += 1
            ph = pid[0]

            def sb(shape, dt, nb=1):
                cnt[0] += 1
                return Tile(st.enter_context(nc.sbuf_tensor("t%d_%d" % (ph, cnt[0]), list(shape), dt)), nb)

            def psums():
                return [Tile(st.enter_context(nc.psum_tensor("ps%d_%d" % (ph, i), [128, 512], F32))) for i in range(8)]

            fn(sb, psums)
            kb.end()

    def fsz(ap):
        n = 1
        for d in ap.shape[1:]:
            n *= d
        return n

    def mm(out_t, out_ap, lhsT, rhs, reads, start=True, stop=True):
        c = (max(120.0, fsz(rhs) * 0.42) + 15.0) * (8.0 if rhs.dtype == F32 else 1.0)
        kb.op("pe", lambda e: e.matmul(out_ap, lhsT, rhs, start=start, stop=stop), reads, [out_t.b], cost=c, lat=120.0)

    def act(out_ap, in_ap, func, reads, writes, small=False, **kw):
        kb.op("act", lambda e: e.activation(out_ap, in_ap, func, **kw), reads, writes, small=small, cost=220.0 + 0.58 * fsz(out_ap), lat=60.0, tab=TABS.get(func))

    def tt(eng, out_ap, in0, in1, op, reads, writes, small=False):
        kb.op(eng, lambda e: e.tensor_tensor(out_ap, in0, in1, op), reads, writes, small=small, cost=(70.0 + 1.05 * fsz(out_ap)) if eng != "pool" else (100.0 + 2.2 * fsz(out_ap)), lat=60.0)

    def ts(eng, out_ap, in0, s1, s2, op0, op1, reads, writes, small=False):
        if op1 is None:
            kb.op(eng, lambda e: e.tensor_scalar(out_ap, in0, s1, None, op0), reads, writes, small=small, cost=(70.0 + 1.05 * fsz(out_ap)) if eng != "pool" else 7400.0, lat=60.0)
        else:
            kb.op(eng, lambda e: e.tensor_scalar(out_ap, in0, s1, s2, op0, op1), reads, writes, small=small, cost=(70.0 + 1.05 * fsz(out_ap)) if eng != "pool" else 7400.0, lat=60.0)

    def stt(eng, out_ap, in0, s, in1, op0, op1, reads, writes):
        kb.op(eng, lambda e: e.scalar_tensor_tensor(out_ap, in0, s, in1, op0, op1), reads, writes, cost=240.0 + 1.1 * fsz(out_ap), lat=60.0)

    def cp(eng, out_ap, in_ap, reads, writes, small=False):
        kb.op(eng, lambda e: e.tensor_copy(out_ap, in_ap), reads, writes, small=small, cost=(70.0 + 0.8 * fsz(out_ap)) if eng != "pool" else (100.0 + 3.4 * fsz(out_ap)), lat=60.0)

    class Rot:
        def __init__(self, items):
            self.items = items
            self.i = 0

        def next(self):
            x = self.items[self.i % len(self.items)]
            self.i += 1
            return x

    def load_consts(sb, names):
        out = {}
        dsem = kb.new_dsem()
        for nm, c0, c1, dt, rows in names:
            f = sb([128, c1 - c0], F32)
            kb.dma(f.t[0:rows, :], cst_in[0:rows, c0:c1], [], [f.b], dsem)
            if dt == BF16:
                g = sb([128, c1 - c0], BF16)
                cp("dve", g.t[0:rows, :], f.t[0:rows, :], [f.b], [g.b])
                out[nm] = g
            else:
                out[nm] = f
        return out

    def load_w_bf16(dst, dst_ap_fn, src_ap_fn, nchunks, width, stage, sdsem, engs):
        for kc in range(nchunks):
            s = stage.next()
            dsm = sdsem[id(s)]
            kb.dma(s.t[:, 0:width], src_ap_fn(kc), [], [s.b], dsm)
            eng = engs.next()
            if eng == "act":
                act(dst_ap_fn(kc), s.t[:, 0:width], AF.Copy, [s.b], [dst.b])
            else:
                cp(eng, dst_ap_fn(kc), s.t[:, 0:width], [s.b], [dst.b])

    def p0(sb, psums):
        cs = load_consts(sb, [("inv", C_INV, C_INV + 1, F32, 128)])
        inv = cs["inv"]
        TWO_PI = 2.0 * math.pi
        C1 = 6.28125
        C2 = TWO_PI - C1
        PI_S = 3.1415925
        pi_ = [sb([128, TT], I32) for _ in range(2)]
        pi_d = [kb.new_dsem() for _ in range(2)]
        ang = sb([128, TT], F32)
        a2 = Rot([sb([128, TT], F32) for _ in range(2)])
        ki = sb([128, TT], I32)
        kf = sb([128, TT], F32)
        sn = [sb([128, TT], F32) for _ in range(2)]
        sn_d = [kb.new_dsem() for _ in range(2)]
        xf = [sb([128, 8, TT], F32) for _ in range(2)]
        xf_d = [kb.new_dsem() for _ in range(2)]
        xo = [sb([128, 8, TT], BF16) for _ in range(2)]
        xo_d = [kb.new_dsem() for _ in range(2)]
        for t in range(NT):
            i = t % 2
            sl = slice(t * TT, (t + 1) * TT)
            kb.dma(pi_[i].t[:], pos_in[0:1, sl].broadcast_to([128, TT]), [], [pi_[i].b], pi_d[i])
            kb.dma(xf[i].t[:], xT_in[:, sl].rearrange("(c p) t -> p c t", p=128), [], [xf[i].b], xf_d[i])
            cp("dve", ang.t[:], pi_[i].t[:], [pi_[i].b], [ang.b])
            ts("dve", ang.t[:], ang.t[:], inv.t[:, 0:1], None, ALU.mult, None, [ang.b, inv.b], [ang.b])
            for k, (shift, dst) in enumerate(((0.0, SINd), (0.5 * math.pi, COSd))):
                a = a2.next()
                ts("dve", a.t[:], ang.t[:], shift, None, ALU.add, None, [ang.b], [a.b])
                ts("dve", ki.t[:], a.t[:], 1.0 / TWO_PI, None, ALU.mult, None, [a.b], [ki.b])
                cp("dve", kf.t[:], ki.t[:], [ki.b], [kf.b])
                stt("dve", a.t[:], kf.t[:], -C1, a.t[:], ALU.mult, ALU.add, [kf.b, a.b], [a.b])
                stt("dve", a.t[:], kf.t[:], -C2, a.t[:], ALU.mult, ALU.add, [kf.b, a.b], [a.b])
                ts("dve", a.t[:], a.t[:], -PI_S, PI_S, ALU.max, ALU.min, [a.b], [a.b])
                act(sn[k].t[:], a.t[:], AF.Sin, [a.b], [sn[k].b])
                kb.dma(dst[:, sl], sn[k].t[:], [sn[k].b], [Buf()], sn_d[k])
            for kc in range(8):
                cp(("pool", "act")[kc % 2] if False else "pool", xo[i].t[:, kc, :], xf[i].t[:, kc, :], [xf[i].b], [xo[i].b])
            kb.dma(XTb[:, sl].rearrange("(c p) t -> p c t", p=128), xo[i].t[:], [xo[i].b], [Buf()], xo_d[i])

    def p1a(l, sb, psums):
        ps = psums()
        pr = Rot(ps)
        cs = load_consts(sb, [("one", C_ONE, C_ONE + 128, BF16, 128)])
        ones = cs["one"]
        NA = 928
        Win = sb([128, 8, NA + 32], BF16)
        stage = Rot([sb([128, 1024], F32) for _ in range(3)])
        sdsem = {id(s): kb.new_dsem() for s in stage.items}
        engs = Rot(["pool", "dve", "act"])
        load_w_bf16(Win, lambda kc: Win.t[:, kc, 0:NA], lambda kc: w_in[l, kc * 128:(kc + 1) * 128, 0:NA], 8, NA, stage, sdsem, engs)
        for kc in range(8):
            ts("dve", Win.t[:, kc, NA:NA + 16], Win.t[:, kc, O_KR + 16:O_KR + 32], -1.0, None, ALU.mult, None, [Win.b], [Win.b])
            cp("dve", Win.t[:, kc, NA + 16:NA + 32], Win.t[:, kc, O_KR:O_KR + 16], [Win.b], [Win.b])
        Wqn = sb([128, 2, 512], BF16)
        load_w_bf16(Wqn, lambda kc: Wqn.t[:, kc, :], lambda kc: w_uqn[l, kc * 128:(kc + 1) * 128, :], 2, 512, stage, sdsem, engs)
        Wqr = sb([128, 2, 256], BF16)
        load_w_bf16(Wqr, lambda kc: Wqr.t[:, kc, :], lambda kc: w_uqr[l, kc * 128:(kc + 1) * 128, :], 2, 256, stage, sdsem, engs)
        Wqx = sb([128, 2, 256], BF16)
        for kc in range(2):
            for h in range(8):
                ts("dve", Wqx.t[:, kc, 32 * h:32 * h + 16], Wqr.t[:, kc, 32 * h + 16:32 * h + 32], -1.0, None, ALU.mult, None, [Wqr.b], [Wqx.b])
                cp("dve", Wqx.t[:, kc, 32 * h + 16:32 * h + 32], Wqr.t[:, kc, 32 * h:32 * h + 16], [Wqr.b], [Wqx.b])
        Wk = sb([128, 512], BF16)
        load_w_bf16(Wk, lambda kc: Wk.t[:, :], lambda kc: w_uk[l, :, :], 1, 512, stage, sdsem, engs)
        Wv = sb([128, 512], BF16)
        load_w_bf16(Wv, lambda kc: Wv.t[:, :], lambda kc: w_uv[l, :, :], 1, 512, stage, sdsem, engs)
        colA = sb([128, 16], F32)
        kb.dma(colA.t[:], colsA[l], [], [colA.b], kb.new_dsem())
        eps_t = sb([128, 1], F32)
        kb.op("pool", lambda e: e.memset(eps_t.t[:], EPS), [], [eps_t.b])

        xb = [sb([128, 8, TT], BF16) for _ in range(2)]
        xb_d = [kb.new_dsem() for _ in range(2)]
        cosb = [sb([128, TT], F32) for _ in range(2)]
        sinb = [sb([128, TT], F32) for _ in range(2)]
        cs_d = [kb.new_dsem() for _ in range(2)]
        cosS = sb([128, TT], F32)
        sinS = sb([128, TT], F32)
        xs = sb([128, 2, TT], F32)
        sq = sb([128, 2, TT], BF16)
        rstd = sb([128, TT], F32)
        cqn = sb([128, 2, TT], BF16)
        ckvn = sb([128, TT], BF16)
        ob16 = Rot([sb([128, TT], BF16) for _ in range(6)])
        of32 = Rot([sb([128, TT], F32) for _ in range(6)])
        st_d = {}

        def store(dst_ap, tile, src_ap):
            d = st_d.setdefault(id(tile), kb.new_dsem())
            kb.dma(dst_ap, src_ap, [tile.b], [Buf()], d)

        def issue_loads(t):
            i = t % 2
            sl = slice(t * TT, (t + 1) * TT)
            kb.dma(xb[i].t[:], XTb[:, sl].rearrange("(c p) t -> p c t", p=128), [], [xb[i].b], xb_d[i])
            kb.dma(cosb[i].t[:], COSd[:, sl], [], [cosb[i].b], cs_d[i])
            kb.dma(sinb[i].t[:], SINd[:, sl], [], [sinb[i].b], cs_d[i])

        def rms_feat(chunks_ps, n_feat, gain_cols, out_t, out_aps):
            nck = len(chunks_ps)
            for c, p in enumerate(chunks_ps):
                act(xs.t[:, c, :], p.t[:], AF.Copy, [p.b], [xs.b])
                act(sq.t[:, c, :], p.t[:], AF.Square, [p.b], [sq.b])
            ss = pr.next()
            for c in range(nck):
                mm(ss, ss.t[:], ones.t[:, :], sq.t[:, c, :], [ones.b, sq.b], start=(c == 0), stop=(c == nck - 1))
            act(rstd.t[:], ss.t[:], AF.Ln, [ss.b, eps_t.b], [rstd.b], bias=eps_t.t[:, 0:1], scale=1.0 / n_feat)
            act(rstd.t[:], rstd.t[:], AF.Exp, [rstd.b], [rstd.b], scale=-0.5)
            for c in range(nck):
                stt("dve", out_aps[c], xs.t[:, c, :], gain_cols[c], rstd.t[:], ALU.mult, ALU.mult, [xs.b, rstd.b, colA.b], [out_t.b])

        issue_loads(0)
        for t in range(NT):
            i = t % 2
            sl = slice(t * TT, (t + 1) * TT)
            if t + 1 < NT:
                issue_loads(t + 1)
            X = xb[i]

            def proj(cols, M):
                p = pr.next()
                for kc in range(8):
                    mm(p, p.t[0:M, :], Win.t[:, kc, cols:cols + M], X.t[:, kc, :], [Win.b, X.b], start=(kc == 0), stop=(kc == 7))
                return p

            act(cosS.t[:], cosb[i].t[:], AF.Copy, [cosb[i].b], [cosS.b], scale=A_SCALE)
            act(sinS.t[:], sinb[i].t[:], AF.Copy, [sinb[i].b], [sinS.b], scale=A_SCALE)
            cq_ps = [proj(O_CQ + 128 * c, 128) for c in range(2)]
            rms_feat(cq_ps, 256, [colA.t[:, 0:1], colA.t[:, 1:2]], cqn, [cqn.t[:, 0, :], cqn.t[:, 1, :]])
            for ch in range(4):
                p = pr.next()
                for kc in range(2):
                    mm(p, p.t[:], Wqn.t[:, kc, ch * 128:(ch + 1) * 128], cqn.t[:, kc, :], [Wqn.b, cqn.b], start=(kc == 0), stop=(kc == 1))
                o = ob16.next()
                act(o.t[:], p.t[:], AF.Copy, [p.b], [o.b], scale=A_SCALE)
                for hh in range(2):
                    store(Qd[2 * ch + hh, 0:64, sl], o, o.t[64 * hh:64 * hh + 64, :])
            for ch in range(2):
                p1_ = pr.next()
                for kc in range(2):
                    mm(p1_, p1_.t[:], Wqr.t[:, kc, ch * 128:(ch + 1) * 128], cqn.t[:, kc, :], [Wqr.b, cqn.b], start=(kc == 0), stop=(kc == 1))
                p2_ = pr.next()
                for kc in range(2):
                    mm(p2_, p2_.t[:], Wqx.t[:, kc, ch * 128:(ch + 1) * 128], cqn.t[:, kc, :], [Wqx.b, cqn.b], start=(kc == 0), stop=(kc == 1))
                t1 = of32.next()
                t2 = of32.next()
                tt("dve", t1.t[:], p1_.t[:], cosS.t[:], ALU.mult, [p1_.b, cosS.b], [t1.b])
                tt("dve", t2.t[:], p2_.t[:], sinS.t[:], ALU.mult, [p2_.b, sinS.b], [t2.b])
                o = ob16.next()
                tt("pool", o.t[:], t1.t[:], t2.t[:], ALU.add, [t1.b, t2.b], [o.b])
                for hh in range(4):
                    store(Qd[4 * ch + hh, 64:96, sl], o, o.t[32 * hh:32 * hh + 32, :])
            ckv_ps = [proj(O_CKV, 128)]
            rms_feat(ckv_ps, 128, [colA.t[:, 2:3]], ckvn, [ckvn.t[:, :]])
            for ch in range(4):
                p = pr.next()
                mm(p, p.t[:], Wk.t[:, ch * 128:(ch + 1) * 128], ckvn.t[:, :], [Wk.b, ckvn.b])
                o = ob16.next()
                if ch % 2 == 0:
                    act(o.t[:], p.t[:], AF.Copy, [p.b], [o.b])
                else:
                    cp("dve", o.t[:], p.t[:], [p.b], [o.b])
                for hh in range(2):
                    store(Kd[2 * ch + hh, :, sl], o, o.t[64 * hh:64 * hh + 64, :])
            for blk in range(4):
                p = pr.next()
                mm(p, p.t[:], ckvn.t[:, blk * 128:(blk + 1) * 128], Wv.t[:, :], [Wv.b, ckvn.b])
                o = ob16.next()
                if blk % 2 == 0:
                    act(o.t[:], p.t[:], AF.Copy, [p.b], [o.b])
                else:
                    cp("dve", o.t[:], p.t[:], [p.b], [o.b])
                store(Vd[:, :, t * 4 + blk, :].rearrange("h p c -> p h c"), o, o.t[:, :].rearrange("p (h c) -> p h c", c=64))
            pk1 = proj(O_KR, 32)
            pk2 = proj(NA, 32)
            t1 = of32.next()
            t2 = of32.next()
            tt("dve", t1.t[0:32, :], pk1.t[0:32, :], cosb[i].t[0:32, :], ALU.mult, [pk1.b, cosb[i].b], [t1.b])
            tt("dve", t2.t[0:32, :], pk2.t[0:32, :], sinb[i].t[0:32, :], ALU.mult, [pk2.b, sinb[i].b], [t2.b])
            o = ob16.next()
            tt("pool", o.t[0:32, :], t1.t[0:32, :], t2.t[0:32, :], ALU.add, [t1.b, t2.b], [o.b])
            store(KRd[:, sl], o, o.t[0:32, :])
            for c in range(4):
                p = proj(O_ZA + 128 * c, 128)
                o = of32.next()
                act(o.t[:], p.t[:], AF.Silu, [p.b], [o.b])
                store(SZAd[c * 128:(c + 1) * 128, sl], o, o.t[:, :])

    def p2(l, sb, psums, pre=None):
        ps = psums()
        if pre is not None:
            stg = Rot([sb([128, 1028], F32) for _ in range(2)])
            sds = {id(x): kb.new_dsem() for x in stg.items}
            Wn = pre["Win"]
            for part in range(2):
                load_w_bf16(Wn, lambda kc, part=part: Wn.t[:, kc, part * 1028:(part + 1) * 1028],
                            lambda kc, part=part: w_in[l, kc * 128:(kc + 1) * 128, O_QKV + part * 1028:O_QKV + (part + 1) * 1028], 8, 1028, stg, sds, Rot(["pool", "dve"]))
            pre["loaded"] = True
        s_rot = Rot(ps[0:5])
        o_rot = Rot(ps[5:8])
        cs = load_consts(sb, [("cm", C_CM, C_CM + 2048, BF16, 128)])
        cm = cs["cm"]
        Ksb = [sb([96, T], BF16) for _ in range(2)]
        Vraw = [sb([128, NB * 64], BF16) for _ in range(2)]
        Vsb = [sb([128, NB, 128], BF16) for _ in range(2)]
        kv_d = [kb.new_dsem() for _ in range(2)]
        for i in range(2):
            kb.op("pool", lambda e, i=i: e.memset(Vsb[i].t[:, :, 64:128], 1.0), [], [Vsb[i].b])
        NQ = 4
        Qsb = [sb([96, TT], BF16) for _ in range(NQ)]
        Zsb = [sb([64, TT], F32) for _ in range(NQ)]
        q_d = [kb.new_dsem() for _ in range(NQ)]
        Psb = Rot([sb([128, TT], BF16) for _ in range(6)])
        rsb = Rot([sb([64, TT], F32) for _ in range(2)])
        osb = Rot([sb([64, TT], F32) for _ in range(2)])
        ysb = Rot([sb([64, TT], BF16) for _ in range(2)])
        y_d = {}

        def load_kv(h):
            i = h % 2
            kb.dma(Ksb[i].t[0:64, :], Kd[h, :, :], [], [Ksb[i].b], kv_d[i])
            kb.dma(Ksb[i].t[64:96, :], KRd[:, :], [], [Ksb[i].b], kv_d[i])
            kb.dma(Vraw[i].t[:, :], Vd[h].rearrange("p b c -> p (b c)"), [], [Vraw[i].b], kv_d[i])

        seq = [(h, qt) for h in range(8) for qt in range(NT)]

        def load_q(n):
            h, qt = seq[n]
            j = n % NQ
            sl = slice(qt * TT, (qt + 1) * TT)
            kb.dma(Qsb[j].t[:, :], Qd[h, :, sl], [], [Qsb[j].b], q_d[j])
            kb.dma(Zsb[j].t[:, :], SZAd[h * 64:(h + 1) * 64, sl], [], [Zsb[j].b], q_d[j])

        work = []
        for n, (h, qt) in enumerate(seq):
            nkb = 4 * (qt + 1)
            for kbk in range(nkb):
                work.append((n, h, qt, kbk, nkb))
        LA = 3
        load_kv(0)
        load_q(0)
        if len(seq) > 1:
            load_q(1)
        pts = {}
        obank = {}
        for idx in range(len(work) + LA):
            if idx < len(work):
                n, h, qt, kbk, nkb = work[idx]
                i = h % 2
                j = n % NQ
                if kbk == 0:
                    if qt == 0:
                        cp("pool", Vsb[i].t[:, :, 0:64], Vraw[i].t[:, :].rearrange("p (b c) -> p b c", c=64), [Vraw[i].b], [Vsb[i].b])
                        if h + 1 < 8:
                            load_kv(h + 1)
                    if n + 2 < len(seq):
                        load_q(n + 2)
                s_ = s_rot.next()
                mm(s_, s_.t[:], Ksb[i].t[:, kbk * 128:(kbk + 1) * 128], Qsb[j].t[:, :], [Ksb[i].b, Qsb[j].b])
                pt = Psb.next()
                act(pt.t[:], s_.t[:], AF.Exp, [s_.b], [pt.b])
                m = kbk - 4 * qt
                if m >= 0:
                    tt("pool", pt.t[:], pt.t[:], cm.t[:, m * 512:(m + 1) * 512], ALU.mult, [pt.b, cm.b], [pt.b])
                pts[idx] = pt
            k2 = idx - LA
            if k2 >= 0:
                n, h, qt, kbk, nkb = work[k2]
                i = h % 2
                j = n % NQ
                sl = slice(qt * TT, (qt + 1) * TT)
                if kbk == 0:
                    obank[n] = o_rot.next()
                ops_ = obank[n]
                pt = pts.pop(k2)
                mm(ops_, ops_.t[:], Vsb[i].t[:, kbk, :], pt.t[:], [Vsb[i].b, pt.b], start=(kbk == 0), stop=(kbk == nkb - 1))
                if kbk == nkb - 1:
                    r = rsb.next()
                    kb.op("dve", lambda e, r=r, ops_=ops_: e.reciprocal(r.t[:, :], ops_.t[64:128, :]), [ops_.b], [r.b], cost=620.0, lat=60.0)
                    o = osb.next()
                    tt("dve", o.t[:, :], ops_.t[0:64, :], r.t[:, :], ALU.mult, [ops_.b, r.b], [o.b])
                    y = ysb.next()
                    tt("pool", y.t[:, :], o.t[:, :], Zsb[j].t[:, :], ALU.mult, [o.b, Zsb[j].b], [y.b])
                    kb.dma(Yd[0, h * 64:(h + 1) * 64, sl], y.t[:, :], [y.b], [Buf()])
                    del obank[n]

    def pzero(n, sb, psums):
        z = sb([128, TT], BF16)
        kb.op("pool", lambda e: e.memset(z.t[:], 0.0), [], [z.b])
        d = kb.new_dsem()
        for t in range(NT):
            for c in range(4):
                kb.dma(Yd[n, c * 128:(c + 1) * 128, t * TT:(t + 1) * TT], z.t[:], [z.b], [Buf()], d)

    def p1c(l, sb, psums, pre=None):
        ps = psums()
        pr = Rot(ps)
        cs = load_consts(sb, [("one", C_ONE, C_ONE + 128, BF16, 128), ("id", C_ID, C_ID + 128, BF16, 128),
                              ("m32", C_M32, C_M32 + 512, F32, 128), ("c32", C_C32, C_C32 + 32, F32, 32)])
        ones, ident, m32, c32 = cs["one"], cs["id"], cs["m32"], cs["c32"]
        NC_ = 2048
        Win = sb([128, 8, NC_], BF16)
        stage = Rot([sb([128, 1024], F32) for _ in range(3)])
        sdsem = {id(s): kb.new_dsem() for s in stage.items}
        engs = Rot(["pool", "dve", "act"])
        for part in range(2):
            load_w_bf16(Win, lambda kc, part=part: Win.t[:, kc, part * 1024:(part + 1) * 1024],
                        lambda kc, part=part: w_in[l, kc * 128:(kc + 1) * 128, O_HQ + part * 1024:O_HQ + (part + 1) * 1024], 8, 1024, stage, sdsem, engs)
        if pre is not None:
            Wg_, Wbr_ = pre["Wg"], pre["Wbr"]
            pe_ = Rot(["pool"])
            for part in range(3):
                load_w_bf16(Wg_, lambda kc, part=part: Wg_.t[:, kc, part * 1024:(part + 1) * 1024],
                            lambda kc, part=part: w_in[l, kc * 128:(kc + 1) * 128, O_G + part * 1024:O_G + (part + 1) * 1024], 8, 1024, stage, sdsem, pe_)
            load_w_bf16(Wbr_, lambda kc: Wbr_.t[:, kc, :], lambda kc: w_br[l, kc // 4, (kc % 4) * 128:(kc % 4 + 1) * 128, :], 12, D, stage, sdsem, pe_)
            pre["loaded"] = True
        colC = sb([128, 32], F32)
        kb.dma(colC.t[:], colsC[l], [], [colC.b], kb.new_dsem())
        eps_t = sb([128, 1], F32)
        kb.op("pool", lambda e: e.memset(eps_t.t[:], EPS), [], [eps_t.b])
        ex = sb([128, 16], F32)
        act(ex.t[:], colC.t[:, 8:24], AF.Exp, [colC.b], [ex.b], small=True)
        exv = ex.t[:, :].rearrange("p (h j) -> p h j", j=4)
        ssum = sb([128, 4], F32)
        lbt = sb([128, 4], F32)
        oml = sb([128, 4], F32)
        tt("dve", ssum.t[:], exv[:, :, 0], exv[:, :, 1], ALU.add, [ex.b], [ssum.b], small=True)
        tt("dve", ssum.t[:], ssum.t[:], exv[:, :, 2], ALU.add, [ex.b, ssum.b], [ssum.b], small=True)
        tt("dve", ssum.t[:], ssum.t[:], exv[:, :, 3], ALU.add, [ex.b, ssum.b], [ssum.b], small=True)
        kb.op("dve", lambda e: e.memset(lbt.t[:], 0.0), [], [lbt.b], small=True)
        for j in range(1, l + 1):
            tt("dve", lbt.t[:], lbt.t[:], exv[:, :, j], ALU.add, [ex.b, lbt.b], [lbt.b], small=True)
        kb.op("dve", lambda e: e.reciprocal(ssum.t[:], ssum.t[:]), [ssum.b], [ssum.b], small=True)
        tt("dve", lbt.t[:], lbt.t[:], ssum.t[:], ALU.mult, [lbt.b, ssum.b], [lbt.b], small=True)
        ts("dve", lbt.t[:], lbt.t[:], 0.0, 1.0 - 1e-6, ALU.max, ALU.min, [lbt.b], [lbt.b], small=True)
        ts("dve", oml.t[:], lbt.t[:], -1.0, 1.0, ALU.mult, ALU.add, [lbt.b], [oml.b], small=True)
        hom = sb([128, 4], F32)
        ts("dve", hom.t[:], oml.t[:], 0.5, None, ALU.mult, None, [oml.b], [hom.b], small=True)

        if dbg:
            kb.dma(dbg_s[:, 0:4], lbt.t[:, :], [lbt.b], [Buf()], kb.new_dsem())
            kb.dma(dbg_s[:, 4:8], oml.t[:, :], [oml.b], [Buf()], kb.new_dsem())
        xb = [sb([128, 8, TT], BF16) for _ in range(2)]
        xb_d = [kb.new_dsem() for _ in range(2)]
        S = sb([128, 4, 128], F32)
        kb.op("pool", lambda e: e.memset(S.t[:], 0.0), [], [S.b])
        Sb = Rot([sb([128, 128], BF16) for _ in range(6)])
        f32r = Rot([sb([128, TT], F32) for _ in range(14)])
        b16r = Rot([sb([128, TT], BF16) for _ in range(8)])
        tmr = Rot([sb([32, 16, 128], BF16) for _ in range(4)])
        scr = Rot([sb([32, TT], BF16) for _ in range(2)])
        yr = Rot([sb([128, TT], BF16) for _ in range(2)])
        st_d = {}

        def store(dst_ap, tile, src_ap):
            d = st_d.setdefault(id(tile), kb.new_dsem())
            kb.dma(dst_ap, src_ap, [tile.b], [Buf()], d)

        def issue_loads(t):
            i = t % 2
            sl = slice(t * TT, (t + 1) * TT)
            kb.dma(xb[i].t[:], XTb[:, sl].rearrange("(c p) t -> p c t", p=128), [], [xb[i].b], xb_d[i])

        issue_loads(0)
        for t in range(NT):
            i = t % 2
            sl = slice(t * TT, (t + 1) * TT)
            if t + 1 < NT:
                issue_loads(t + 1)
            X = xb[i]

            def proj(cols):
                p = pr.next()
                for kc in range(8):
                    mm(p, p.t[:], Win.t[:, kc, cols:cols + 128], X.t[:, kc, :], [Win.b, X.b], start=(kc == 0), stop=(kc == 7))
                return p

            for h in range(4):
                hq_ps = proj(128 * h)
                qf = f32r.next()
                act(qf.t[:], hq_ps.t[:], AF.Silu, [hq_ps.b], [qf.b])
                hf_ps = proj(512 + 128 * h)
                sig = f32r.next()
                act(sig.t[:], hf_ps.t[:], AF.Tanh, [hf_ps.b], [sig.b], scale=0.5)
                hi_ps = proj(1024 + 128 * h)
                vT = b16r.next()
                act(vT.t[:], hi_ps.t[:], AF.Copy, [hi_ps.b], [vT.b])
                zc_ps = proj(1536 + 128 * h)
                szc = f32r.next()
                act(szc.t[:], zc_ps.t[:], AF.Silu, [zc_ps.b], [szc.b])
                tq = f32r.next()
                ts("dve", tq.t[:], sig.t[:], hom.t[:, h:h + 1], hom.t[:, h:h + 1], ALU.mult, ALU.add, [sig.b, hom.b], [tq.b])
                ff = f32r.next()
                ts("dve", ff.t[:], tq.t[:], 1e-30, lbt.t[:, h:h + 1], ALU.max, ALU.add, [tq.b, lbt.b], [ff.b])
                act(ff.t[:], ff.t[:], AF.Ln, [ff.b], [ff.b])
                kk = f32r.next()
                ts("dve", kk.t[:], tq.t[:], -1.0, oml.t[:, h:h + 1], ALU.mult, ALU.add, [tq.b, oml.b], [kk.b])
                cum = f32r.next()
                kb.op("dve", lambda e, cum=cum, ff=ff: e.tensor_tensor_scan(cum.t[:], m32.t[:], ff.t[:], 0.0, ALU.mult, ALU.add), [m32.b, ff.b], [cum.b], cost=1250.0, lat=60.0)
                ecum = f32r.next()
                act(ecum.t[:], cum.t[:], AF.Exp, [cum.b], [ecum.b])
                einv = f32r.next()
                act(einv.t[:], cum.t[:], AF.Exp, [cum.b], [einv.b], scale=-1.0)
                qdec = b16r.next()
                stt("dve", qdec.t[:], qf.t[:], 128 ** -0.5, ecum.t[:], ALU.mult, ALU.mult, [qf.b, ecum.b], [qdec.b])
                kinv = b16r.next()
                tt("dve", kinv.t[:], kk.t[:], einv.t[:], ALU.mult, [kk.b, einv.b], [kinv.b])
                dd = f32r.next()
                cv = cum.t[:, :].rearrange("p (c t) -> p c t", t=32)
                tt("pool", dd.t[:, :].rearrange("p (c t) -> p c t", t=32), cv[:, :, 31:32].broadcast_to([128, 16, 32]), cv, ALU.subtract, [cum.b], [dd.b])
                act(dd.t[:], dd.t[:], AF.Exp, [dd.b], [dd.b])
                kdT = b16r.next()
                tt("pool", kdT.t[:], kk.t[:], dd.t[:], ALU.mult, [kk.b, dd.b], [kdT.b])
                kd_tm = tmr.next()
                v_tm = tmr.next()
                for src, dst in ((kdT, kd_tm), (vT, v_tm)):
                    for half in range(2):
                        p = pr.next()
                        pv = p.t[:].bitcast(BF16)
                        for c in range(8):
                            cc = half * 8 + c
                            kb.op("pe", lambda e, pv=pv, src=src, cc=cc, c=c: e.transpose(pv[0:32, c * 128:(c + 1) * 128], src.t[:, cc * 32:(cc + 1) * 32], ident.t[:, :]),
                                  [src.b, ident.b], [p.b], cost=135.0, lat=120.0)
                        dv = dst.t[:, half * 8:(half + 1) * 8, :].rearrange("p c d -> p (c d)")
                        if half == 0:
                            act(dv, pv[0:32, :], AF.Copy, [p.b], [dst.b])
                        else:
                            cp("dve", dv, pv[0:32, :], [p.b], [dst.b])
                sc_ps = pr.next()
                for c in range(16):
                    mm(sc_ps, sc_ps.t[0:32, c * 32:(c + 1) * 32], kinv.t[:, c * 32:(c + 1) * 32], qdec.t[:, c * 32:(c + 1) * 32], [kinv.b, qdec.b])
                scm = scr.next()
                tt("dve", scm.t[:, :].rearrange("p (c t) -> p c t", t=32), sc_ps.t[0:32, :].rearrange("p (c t) -> p c t", t=32),
                   c32.t[0:32, :].unsqueeze(1).broadcast_to([32, 16, 32]), ALU.mult, [sc_ps.b, c32.b], [scm.b])
                o_ps = pr.next()
                dS = None
                for c in range(16):
                    if c % 4 == 0:
                        dS = pr.next()
                        for c2 in range(c, c + 4):
                            mm(dS, dS.t[:, (c2 % 4) * 128:(c2 % 4 + 1) * 128], kd_tm.t[:, c2, :], v_tm.t[:, c2, :], [kd_tm.b, v_tm.b])
                    sbf = Sb.next()
                    cp("dve", sbf.t[:, :], S.t[:, h, :], [S.b], [sbf.b])
                    mm(o_ps, o_ps.t[:, c * 32:(c + 1) * 32], sbf.t[:, :], qdec.t[:, c * 32:(c + 1) * 32], [sbf.b, qdec.b], start=True, stop=False)
                    mm(o_ps, o_ps.t[:, c * 32:(c + 1) * 32], v_tm.t[:, c, :], scm.t[:, c * 32:(c + 1) * 32], [v_tm.b, scm.b], start=False, stop=True)
                    stt("dve", S.t[:, h, :], S.t[:, h, :], ecum.t[:, c * 32 + 31:c * 32 + 32], dS.t[:, (c % 4) * 128:(c % 4 + 1) * 128], ALU.mult, ALU.add,
                        [S.b, ecum.b, dS.b], [S.b])
                osb = f32r.next()
                act(osb.t[:], o_ps.t[:], AF.Copy, [o_ps.b], [osb.b])
                sq = b16r.next()
                act(sq.t[:], o_ps.t[:], AF.Square, [o_ps.b], [sq.b])
                ss = pr.next()
                mm(ss, ss.t[:], ones.t[:, :], sq.t[:, :], [ones.b, sq.b])
                rstd = f32r.next()
                act(rstd.t[:], ss.t[:], AF.Ln, [ss.b, eps_t.b], [rstd.b], bias=eps_t.t[:, 0:1], scale=1.0 / 128)
                act(rstd.t[:], rstd.t[:], AF.Exp, [rstd.b], [rstd.b], scale=-0.5)
                stt("dve", osb.t[:], osb.t[:], colC.t[:, 0:1], rstd.t[:], ALU.mult, ALU.mult, [osb.b, colC.b, rstd.b], [osb.b])
                y = yr.next()
                tt("pool", y.t[:], osb.t[:], szc.t[:], ALU.mult, [osb.b, szc.b], [y.b])
                store(Yd[2, 128 * h:128 * (h + 1), sl], y, y.t[:, :])

    def p1b(l, sb, psums, pre=None):
        ps = psums()
        pr = Rot(ps[0:6])
        cs = load_consts(sb, [("one", C_ONE, C_ONE + 128, BF16, 128), ("id", C_ID, C_ID + 128, BF16, 128),
                              ("idf", C_ID, C_ID + 128, F32, 128),
                              ("m128", C_M128, C_M128 + 512, F32, 128), ("uti", C_UTI, C_UTI + 128, F32, 128),
                              ("uts", C_UTS, C_UTS + 128, F32, 128), ("selb", C_SELB, C_SELB + 1024, F32, 8)])
        ones, ident, identf, m128, uti, uts, selb = cs["one"], cs["id"], cs["idf"], cs["m128"], cs["uti"], cs["uts"], cs["selb"]
        NBW = 2056
        if pre is not None and pre.get("loaded"):
            Win = pre["Win"]
            Win.b = Buf()
        else:
            Win = sb([128, 8, NBW], BF16)
            stage = Rot([sb([128, 1028], F32) for _ in range(2)])
            sdsem = {id(s_): kb.new_dsem() for s_ in stage.items}
            engs = Rot(["pool", "dve", "act"])
            for part in range(2):
                load_w_bf16(Win, lambda kc, part=part: Win.t[:, kc, part * 1028:(part + 1) * 1028],
                            lambda kc, part=part: w_in[l, kc * 128:(kc + 1) * 128, O_QKV + part * 1028:O_QKV + (part + 1) * 1028], 8, 1028, stage, sdsem, engs)
        colC = sb([128, 32], F32)
        kb.dma(colC.t[:], colsC[l], [], [colC.b], kb.new_dsem())
        colW = sb([128, 12, 4], F32)
        kb.dma(colW.t[:], colsW[l], [], [colW.b], kb.new_dsem())
        colG = sb([128, 8], F32)
        kb.dma(colG.t[:], colsG[l], [], [colG.b], kb.new_dsem())
        eps_t = sb([128, 1], F32)
        kb.op("pool", lambda e: e.memset(eps_t.t[:], EPS), [], [eps_t.b])
        one_t = sb([128, 1], F32)
        kb.op("pool", lambda e: e.memset(one_t.t[:], 1.0), [], [one_t.b])
        lnq_t = sb([128, 1], F32)
        kb.op("pool", lambda e: e.memset(lnq_t.t[:], math.log(128 ** -0.5)), [], [lnq_t.b])
        nega = sb([128, 1], F32)
        act(nega.t[0:8, :], colG.t[0:8, 0:1], AF.Exp, [colG.b], [nega.b], small=True)
        ts("dve", nega.t[0:8, :], nega.t[0:8, :], -1.0, None, ALU.mult, None, [nega.b], [nega.b], small=True)

        xb = [sb([128, 8, TT], BF16) for _ in range(2)]
        xb_d = [kb.new_dsem() for _ in range(2)]
        raw = sb([128, 12, 4 + TT], BF16)
        rawb = [Buf() for _ in range(12)]
        for c in range(12):
            kb.op("pool", lambda e, c=c: e.memset(raw.t[:, c, 0:4], 0.0), [], [rawb[c]])
        Dg = sb([128, 12, 4, 128], BF16)
        for c in range(12):
            for j in range(4):
                ts("dve", Dg.t[:, c, j, :], ident.t[:, :], colW.t[:, c, j:j + 1], None, ALU.mult, None, [ident.b, colW.b], [Dg.b])
        qs = [sb([128, TT], F32) for _ in range(8)]
        vT = [sb([128, TT], BF16) for _ in range(4)]
        szb = [sb([128, TT], F32) for _ in range(4)]
        qnb = [sb([128, TT], BF16) for _ in range(4)]
        knb = [sb([128, TT], BF16) for _ in range(4)]
        x8 = sb([8, TT], F32)
        n8 = sb([8, TT], F32)
        G8 = sb([8, TT], F32)
        B8 = sb([8, TT], F32)
        gc8 = sb([8, TT], F32)
        gcT = sb([128, 4, 8], F32)
        bT = sb([128, 4, 8], F32)
        Sg = sb([128, 4, 128], F32)
        kb.op("pool", lambda e: e.memset(Sg.t[:], 0.0), [], [Sg.b])
        Sgb = [sb([128, 128], BF16) for _ in range(4)]
        for h in range(4):
            kb.op("pool", lambda e, h=h: e.memset(Sgb[h].t[:], 0.0), [], [Sgb[h].b])
        NH = 2
        def hset():
            d = {}
            for nm in ("kbT", "qdec", "U", "L", "U2", "L2", "P", "Q", "intraT", "kbg", "kdec", "vb", "wTn", "vn", "sq"):
                d[nm] = sb([128, TT], BF16)
            for nm in ("Es", "Ei", "egcB", "E"):
                d[nm] = sb([128, TT], F32)
            d["osb"] = d["E"]
            d["gcBs"] = sb([128, TT], F32)
            d["rstd"] = d["Es"]
            for nm in ("glast", "eglast", "egcT", "bgT", "kdcol"):
                d[nm] = sb([128, 4], F32)
            return d
        HS = [hset() for _ in range(NH)]
        t32 = Rot([sb([128, TT], F32) for _ in range(2)])
        yr = Rot([sb([128, TT], BF16) for _ in range(2)])
        st_d = {}

        def store(dst_ap, tile, src_ap):
            d = st_d.setdefault(id(tile), kb.new_dsem())
            kb.dma(dst_ap, src_ap, [tile.b], [Buf()], d)

        def issue_loads(t):
            i = t % 2
            sl = slice(t * TT, (t + 1) * TT)
            kb.dma(xb[i].t[:], XTb[:, sl].rearrange("(c p) t -> p c t", p=128), [], [xb[i].b], xb_d[i])

        def b4(tile_ap):
            return tile_ap.unsqueeze(1).broadcast_to([128, 4, 128])

        def v4(ap):
            return ap.rearrange("p (c t) -> p c t", t=128)

        import os
        KSTOP = os.environ.get('KSTOP', '')
        KD = int(os.environ.get('KD', '99'))
        issue_loads(0)
        for t in range(NT):
            i = t % 2
            sl = slice(t * TT, (t + 1) * TT)
            if t + 1 < NT:
                issue_loads(t + 1)
            X = xb[i]

            def proj(cols, M=128):
                p = pr.next()
                for kc in range(8):
                    mm(p, p.t[0:M, :], Win.t[:, kc, cols:cols + M], X.t[:, kc, :], [Win.b, X.b], start=(kc == 0), stop=(kc == 7))
                return p

            for c in range(12):
                p = proj(128 * c)
                if c % 2 == 0:
                    act(raw.t[:, c, 4:4 + TT], p.t[:], AF.Copy, [p.b], [rawb[c]])
                else:
                    cp("dve", raw.t[:, c, 4:4 + TT], p.t[:], [p.b], [rawb[c]])
                cv = pr.next()
                for j in range(4):
                    mm(cv, cv.t[:], Dg.t[:, c, j, :], raw.t[:, c, 1 + j:1 + j + TT], [Dg.b, rawb[c]], start=(j == 0), stop=(j == 3))
                cp("pool", raw.t[:, c, 0:4], raw.t[:, c, TT:TT + 4], [rawb[c]], [rawb[c]], small=True)
                if c < 8:
                    act(qs[c].t[:], cv.t[:], AF.Silu, [cv.b], [qs[c].b])
                else:
                    act(vT[c - 8].t[:], cv.t[:], AF.Silu, [cv.b], [vT[c - 8].b])
            for h in range(4):
                p = proj(1544 + 128 * h)
                act(szb[h].t[:], p.t[:], AF.Silu, [p.b], [szb[h].b])
            if KSTOP == 'A':
                break
            for c in range(8):
                sqt = HS[c % NH]["sq"]
                tt("pool", sqt.t[:], qs[c].t[:], qs[c].t[:], ALU.mult, [qs[c].b], [sqt.b])
                ss = pr.next()
                mm(ss, ss.t[:], ones.t[:, :], sqt.t[:, :], [ones.b, sqt.b])
                rs = t32.next()
                act(rs.t[:], ss.t[:], AF.Ln, [ss.b, eps_t.b], [rs.b], bias=eps_t.t[:, 0:1], scale=1.0)
                if c < 4:
                    act(rs.t[:], rs.t[:], AF.Exp, [rs.b, lnq_t.b], [rs.b], scale=-0.5, bias=lnq_t.t[:, 0:1])
                else:
                    act(rs.t[:], rs.t[:], AF.Exp, [rs.b], [rs.b], scale=-0.5)
                tt("dve", qs[c].t[:], qs[c].t[:], rs.t[:], ALU.mult, [qs[c].b, rs.b], [qs[c].b])
                dstb = qnb[c] if c < 4 else knb[c - 4]
                cp("pool", dstb.t[:], qs[c].t[:], [qs[c].b], [dstb.b])
            if KSTOP == 'B':
                break
            ab = proj(1536, 8)
            act(B8.t[:, :], ab.t[0:8, :], AF.Tanh, [ab.b], [B8.b], scale=0.5)
            ts("dve", B8.t[:, :], B8.t[:, :], 0.5, 0.5, ALU.mult, ALU.add, [B8.b], [B8.b])
            act(x8.t[:, :], ab.t[0:8, :], AF.Copy, [ab.b], [x8.b])
            ts("dve", x8.t[:, :], x8.t[:, :], colG.t[0:8, 1:2], None, ALU.add, None, [x8.b, colG.b], [x8.b])
            ts("dve", n8.t[:, :], x8.t[:, :], -1.0, None, ALU.mult, None, [x8.b], [n8.b])
            tt("dve", n8.t[:, :], n8.t[:, :], x8.t[:, :], ALU.min, [n8.b, x8.b], [n8.b])
            act(n8.t[:, :], n8.t[:, :], AF.Exp, [n8.b], [n8.b])
            act(n8.t[:, :], n8.t[:, :], AF.Ln, [n8.b, one_t.b], [n8.b], bias=one_t.t[0:8, 0:1], scale=1.0)
            stt("dve", G8.t[:, :], x8.t[:, :], 0.0, n8.t[:, :], ALU.max, ALU.add, [x8.b, n8.b], [G8.b])
            ts("dve", G8.t[:, :], G8.t[:, :], nega.t[0:8, 0:1], None, ALU.mult, None, [G8.b, nega.b], [G8.b])
            kb.op("dve", lambda e: e.tensor_tensor_scan(gc8.t[:, :], m128.t[0:8, :], G8.t[:, :], 0.0, ALU.mult, ALU.add), [m128.b, G8.b], [gc8.b])
            pT_ = pr.next()
            for ck in range(4):
                mm(pT_, pT_.t[:, ck * 8:(ck + 1) * 8], gc8.t[:, ck * 128:(ck + 1) * 128], identf.t[0:8, 0:8], [gc8.b, identf.b])
                mm(pT_, pT_.t[:, 32 + ck * 8:32 + (ck + 1) * 8], B8.t[:, ck * 128:(ck + 1) * 128], identf.t[0:8, 0:8], [B8.b, identf.b])
            cp("dve", gcT.t[:, :, :].rearrange("p c e -> p (c e)"), pT_.t[:, 0:32], [pT_.b], [gcT.b], small=True)
            cp("dve", bT.t[:, :, :].rearrange("p c e -> p (c e)"), pT_.t[:, 32:64], [pT_.b], [bT.b], small=True)

            if KSTOP == 'C':
                break
            for pair in range(2):
                hs = [2 * pair, 2 * pair + 1]
                for n, h in enumerate(hs):
                    W_ = HS[n]
                    gcB = pr.next()
                    if KD >= 1:
                        mm(gcB, gcB.t[:], selb.t[0:8, h * 128:(h + 1) * 128], gc8.t[:, :], [selb.b, gc8.b])
                    W_["gcB"] = gcB
                    if KD >= 2:
                        act(W_["egcB"].t[:], gcB.t[:], AF.Exp, [gcB.b], [W_["egcB"].b])
                    if KD >= 3:
                        act(W_["gcBs"].t[:], gcB.t[:], AF.Copy, [gcB.b], [W_["gcBs"].b])
                        cp("dve", W_["glast"].t[:, :], W_["gcBs"].t[:, 127:TT:128], [W_["gcBs"].b], [W_["glast"].b], small=True)
                    if KD >= 4:
                        act(W_["eglast"].t[:, :], W_["glast"].t[:, :], AF.Exp, [W_["glast"].b], [W_["eglast"].b], small=True)
                    if KD >= 5:
                        act(W_["egcT"].t[:, :], gcT.t[:, :, h], AF.Exp, [gcT.b], [W_["egcT"].b], small=True)
                    if KD >= 6:
                        tt("dve", W_["bgT"].t[:, :], W_["egcT"].t[:, :], bT.t[:, :, 4 + h], ALU.mult, [W_["egcT"].b, bT.b], [W_["bgT"].b], small=True)
                    if KD >= 7:
                        tt("dve", W_["kdcol"].t[:, :], W_["glast"].t[:, :], gcT.t[:, :, h], ALU.subtract, [W_["glast"].b, gcT.b], [W_["kdcol"].b], small=True)
                    if KD >= 8:
                        act(W_["kdcol"].t[:, :], W_["kdcol"].t[:, :], AF.Exp, [W_["kdcol"].b], [W_["kdcol"].b], small=True)
                    if KD >= 9:
                        tt("dve", W_["qdec"].t[:], qs[h].t[:], W_["egcB"].t[:], ALU.mult, [qs[h].b, W_["egcB"].b], [W_["qdec"].b])
                    bB = pr.next()
                    if KD >= 10:
                        mm(bB, bB.t[:], selb.t[0:8, (4 + h) * 128:(5 + h) * 128], B8.t[:, :], [selb.b, B8.b])
                    if KD >= 11:
                        tt("dve", W_["kbT"].t[:], bB.t[:], qs[4 + h].t[:], ALU.mult, [bB.b, qs[4 + h].b], [W_["kbT"].b])
                    if KSTOP == 'D':
                        continue
                    for ck in range(4):
                        c_ = slice(ck * 128, (ck + 1) * 128)
                        ts("dve", W_["E"].t[:, c_], W_["gcBs"].t[:, c_], gcT.t[:, ck, h:h + 1], 0.0, ALU.subtract, ALU.min, [W_["gcBs"].b, gcT.b], [W_["E"].b])
                    act(W_["E"].t[:], W_["E"].t[:], AF.Exp, [W_["E"].b], [W_["E"].b])
                    tt("pool", v4(W_["Es"].t[:, :]), v4(W_["E"].t[:, :]), b4(uts.t[:, :]), ALU.mult, [W_["E"].b, uts.b], [W_["Es"].b])
                    tt("pool", v4(W_["Ei"].t[:, :]), v4(W_["E"].t[:, :]), b4(uti.t[:, :]), ALU.mult, [W_["E"].b, uti.b], [W_["Ei"].b])
                    kk = pr.next()
                    for ck in range(4):
                        c_ = slice(ck * 128, (ck + 1) * 128)
                        mm(kk, kk.t[:, c_], knb[h].t[:, c_], W_["kbT"].t[:, c_], [knb[h].b, W_["kbT"].b])
                    tt("dve", W_["U"].t[:], kk.t[:], W_["Es"].t[:], ALU.mult, [kk.b, W_["Es"].b], [W_["U"].b])
                    qk = pr.next()
                    for ck in range(4):
                        c_ = slice(ck * 128, (ck + 1) * 128)
                        mm(qk, qk.t[:, c_], knb[h].t[:, c_], qnb[h].t[:, c_], [knb[h].b, qnb[h].b])
                    tt("dve", W_["intraT"].t[:], qk.t[:], W_["Ei"].t[:], ALU.mult, [qk.b, W_["Ei"].b], [W_["intraT"].b])
                    if KSTOP == 'E':
                        continue
                    lp = pr.next()
                    lpv = lp.t[:].bitcast(BF16)
                    for ck in range(4):
                        c_ = slice(ck * 128, (ck + 1) * 128)
                        kb.op("pe", lambda e, lpv=lpv, W_=W_, c_=c_: e.transpose(lpv[:, c_], W_["U"].t[:, c_], ident.t[:, :]), [W_["U"].b, ident.b], [lp.b], cost=135.0, lat=120.0)
                    act(W_["L"].t[:], lpv[:, 0:TT], AF.Copy, [lp.b], [W_["L"].b])
                    tt("pool", v4(W_["P"].t[:, :]), b4(ident.t[:, :]), v4(W_["U"].t[:, :]), ALU.subtract, [W_["U"].b, ident.b], [W_["P"].b])
                if KSTOP in ('D', 'E', 'F'):
                    break
                cur = [("U", "L") for _ in hs]
                nxt = [("U2", "L2") for _ in hs]
                for s_ in range(1, 7):
                    for n, h in enumerate(hs):
                        W_ = HS[n]
                        Uc, Lc = W_[cur[n][0]], W_[cur[n][1]]
                        Un, Ln = W_[nxt[n][0]], W_[nxt[n][1]]
                        if s_ < 6:
                            pu = pr.next()
                            for ck in range(4):
                                c_ = slice(ck * 128, (ck + 1) * 128)
                                mm(pu, pu.t[:, c_], Lc.t[:, c_], Uc.t[:, c_], [Lc.b, Uc.b])
                            act(Un.t[:], pu.t[:], AF.Copy, [pu.b], [Un.b])
                        pl = pr.next()
                        for ck in range(4):
                            c_ = slice(ck * 128, (ck + 1) * 128)
                            mm(pl, pl.t[:, c_], Uc.t[:, c_], Lc.t[:, c_], [Lc.b, Uc.b])
                        cp("dve", Ln.t[:], pl.t[:], [pl.b], [Ln.b])
                        pp_ = pr.next()
                        for ck in range(4):
                            c_ = slice(ck * 128, (ck + 1) * 128)
                            mm(pp_, pp_.t[:, c_], Ln.t[:, c_], W_["P"].t[:, c_], [W_["P"].b, Ln.b])
                        tt("dve", W_["P"].t[:], pp_.t[:], W_["P"].t[:], ALU.add, [pp_.b, W_["P"].b], [W_["P"].b])
                    cur, nxt = nxt, cur
                if KSTOP == 'G':
                    break
                for n, h in enumerate(hs):
                    W_ = HS[n]
                    kp = pr.next()
                    kpv = kp.t[:].bitcast(BF16)
                    vp = pr.next()
                    vpv = vp.t[:].bitcast(BF16)
                    for ck in range(4):
                        c_ = slice(ck * 128, (ck + 1) * 128)
                        kb.op("pe", lambda e, kpv=kpv, h=h, c_=c_: e.transpose(kpv[:, c_], knb[h].t[:, c_], ident.t[:, :]), [knb[h].b, ident.b], [kp.b], cost=135.0, lat=120.0)
                        kb.op("pe", lambda e, vpv=vpv, h=h, c_=c_: e.transpose(vpv[:, c_], vT[h].t[:, c_], ident.t[:, :]), [vT[h].b, ident.b], [vp.b], cost=135.0, lat=120.0)
                    for ck in range(4):
                        c_ = slice(ck * 128, (ck + 1) * 128)
                        act(W_["kbg"].t[:, c_], kpv[:, c_], AF.Copy, [kp.b, W_["bgT"].b], [W_["kbg"].b], scale=W_["bgT"].t[:, ck:ck + 1])
                        act(W_["kdec"].t[:, c_], kpv[:, c_], AF.Copy, [kp.b, W_["kdcol"].b], [W_["kdec"].b], scale=W_["kdcol"].t[:, ck:ck + 1])
                        ts("dve", W_["vb"].t[:, c_], vpv[:, c_], bT.t[:, ck, 4 + h:5 + h], None, ALU.mult, None, [vp.b, bT.b], [W_["vb"].b])
                    wp = pr.next()
                    for ck in range(4):
                        c_ = slice(ck * 128, (ck + 1) * 128)
                        mm(wp, wp.t[:, c_], W_["kbg"].t[:, c_], W_["P"].t[:, c_], [W_["kbg"].b, W_["P"].b])
                    act(W_["wTn"].t[:], wp.t[:], AF.Copy, [wp.b], [W_["wTn"].b], scale=-1.0)
                if KSTOP == 'H':
                    break
                ops_ = [ps[6], ps[7]]
                for ck in range(4):
                    c_ = slice(ck * 128, (ck + 1) * 128)
                    vps = []
                    for n, h in enumerate(hs):
                        W_ = HS[n]
                        vn_ps = pr.next()
                        mm(vn_ps, vn_ps.t[:, 0:128], W_["P"].t[:, c_], W_["vb"].t[:, c_], [W_["P"].b, W_["vb"].b], start=True, stop=False)
                        mm(vn_ps, vn_ps.t[:, 0:128], W_["wTn"].t[:, c_], Sgb[h].t[:, :], [W_["wTn"].b, Sgb[h].b], start=False, stop=True)
                        vps.append(vn_ps)
                    for n, h in enumerate(hs):
                        W_ = HS[n]
                        act(W_["vn"].t[:, c_], vps[n].t[:, 0:128], AF.Copy, [vps[n].b], [W_["vn"].b])
                    dps = []
                    for n, h in enumerate(hs):
                        W_ = HS[n]
                        mm(ops_[n], ops_[n].t[:, c_], Sgb[h].t[:, :], W_["qdec"].t[:, c_], [Sgb[h].b, W_["qdec"].b], start=True, stop=False)
                        mm(ops_[n], ops_[n].t[:, c_], W_["vn"].t[:, c_], W_["intraT"].t[:, c_], [W_["vn"].b, W_["intraT"].b], start=False, stop=True)
                        d_ps = pr.next()
                        mm(d_ps, d_ps.t[:, 0:128], W_["kdec"].t[:, c_], W_["vn"].t[:, c_], [W_["kdec"].b, W_["vn"].b])
                        dps.append(d_ps)
                    for n, h in enumerate(hs):
                        W_ = HS[n]
                        stt("dve", Sg.t[:, h, :], Sg.t[:, h, :], W_["eglast"].t[:, ck:ck + 1], dps[n].t[:, 0:128], ALU.mult, ALU.add,
                            [Sg.b, W_["eglast"].b, dps[n].b], [Sg.b])
                        cp("dve", Sgb[h].t[:, :], Sg.t[:, h, :], [Sg.b], [Sgb[h].b])
                if KSTOP == 'I':
                    break
                for n, h in enumerate(hs):
                    W_ = HS[n]
                    o_ps = ops_[n]
                    act(W_["osb"].t[:], o_ps.t[:], AF.Copy, [o_ps.b], [W_["osb"].b])
                    act(W_["sq"].t[:], o_ps.t[:], AF.Square, [o_ps.b], [W_["sq"].b])
                    ss = pr.next()
                    mm(ss, ss.t[:], ones.t[:, :], W_["sq"].t[:, :], [ones.b, W_["sq"].b])
                    act(W_["rstd"].t[:], ss.t[:], AF.Ln, [ss.b, eps_t.b], [W_["rstd"].b], bias=eps_t.t[:, 0:1], scale=1.0 / 128)
                    act(W_["rstd"].t[:], W_["rstd"].t[:], AF.Exp, [W_["rstd"].b], [W_["rstd"].b], scale=-0.5)
                    stt("dve", W_["osb"].t[:], W_["osb"].t[:], colC.t[:, 1:2], W_["rstd"].t[:], ALU.mult, ALU.mult, [W_["osb"].b, colC.b, W_["rstd"].b], [W_["osb"].b])
                    y = yr.next()
                    tt("pool", y.t[:], W_["osb"].t[:], szb[h].t[:], ALU.mult, [W_["osb"].b, szb[h].b], [y.b])
                    store(Yd[1, 128 * h:128 * (h + 1), sl], y, y.t[:, :])

    def p3a(l, sb, psums, pre=None):
        ps = psums()
        pr = Rot(ps)
        stage = Rot([sb([128, 1024], F32) for _ in range(3)])
        sdsem = {id(s): kb.new_dsem() for s in stage.items}
        engs = Rot(["pool", "dve", "act"])
        if pre is not None and pre.get("loaded"):
            Wg, Wbr = pre["Wg"], pre["Wbr"]
            Wg.b = Buf()
            Wbr.b = Buf()
        else:
            Wg = sb([128, 8, 3072], BF16)
            for part in range(3):
                load_w_bf16(Wg, lambda kc, part=part: Wg.t[:, kc, part * 1024:(part + 1) * 1024],
                            lambda kc, part=part: w_in[l, kc * 128:(kc + 1) * 128, O_G + part * 1024:O_G + (part + 1) * 1024], 8, 1024, stage, sdsem, engs)
            Wbr = sb([128, 12, D], BF16)
            load_w_bf16(Wbr, lambda kc: Wbr.t[:, kc, :], lambda kc: w_br[l, kc // 4, (kc % 4) * 128:(kc % 4 + 1) * 128, :], 12, D, stage, sdsem, engs)
        xb = [sb([128, 8, TT], BF16) for _ in range(2)]
        yb = [sb([128, 12, TT], BF16) for _ in range(2)]
        ld = [kb.new_dsem() for _ in range(2)]
        mg = [sb([128, 8, TT], BF16) for _ in range(2)]
        mg_d = [kb.new_dsem() for _ in range(2)]
        mf = Rot([sb([128, TT], F32) for _ in range(2)])
        gsb = Rot([sb([128, TT], F32) for _ in range(3)])
        tsb = Rot([sb([128, TT], F32) for _ in range(3)])

        def issue_loads(t):
            i = t % 2
            sl = slice(t * TT, (t + 1) * TT)
            kb.dma(xb[i].t[:], XTb[:, sl].rearrange("(c p) t -> p c t", p=128), [], [xb[i].b], ld[i])
            for n in range(3):
                kb.dma(yb[i].t[:, 4 * n:4 * n + 4, :], Yd[n, :, sl].rearrange("(c p) t -> p c t", p=128), [], [yb[i].b], ld[i])

        issue_loads(0)
        for t in range(NT):
            i = t % 2
            sl = slice(t * TT, (t + 1) * TT)
            if t + 1 < NT:
                issue_loads(t + 1)
            for dc in range(8):
                m_ = mf.next()
                for n in range(3):
                    gp = pr.next()
                    for kc in range(8):
                        mm(gp, gp.t[:], Wg.t[:, kc, n * D + dc * 128:n * D + (dc + 1) * 128], xb[i].t[:, kc, :], [Wg.b, xb[i].b], start=(kc == 0), stop=(kc == 7))
                    g = gsb.next()
                    act(g.t[:], gp.t[:], AF.Sigmoid, [gp.b], [g.b])
                    pp_ = pr.next()
                    for kc in range(4):
                        mm(pp_, pp_.t[:], Wbr.t[:, 4 * n + kc, dc * 128:(dc + 1) * 128], yb[i].t[:, 4 * n + kc, :], [Wbr.b, yb[i].b], start=(kc == 0), stop=(kc == 3))
                    if n == 0:
                        tt("dve", m_.t[:], pp_.t[:], g.t[:], ALU.mult, [pp_.b, g.b], [m_.b])
                    else:
                        tm = tsb.next()
                        tt("dve", tm.t[:], pp_.t[:], g.t[:], ALU.mult, [pp_.b, g.b], [tm.b])
                        if n == 1:
                            tt("pool", m_.t[:], m_.t[:], tm.t[:], ALU.add, [m_.b, tm.b], [m_.b])
                        else:
                            tt("pool", mg[i].t[:, dc, :], m_.t[:], tm.t[:], ALU.add, [m_.b, tm.b], [mg[i].b])
            kb.dma(MGd[:, sl].rearrange("(c p) t -> p c t", p=128), mg[i].t[:], [mg[i].b], [Buf()], mg_d[i])

    def p3b(l, sb, psums):
        src_x = xT_in if l == 0 else XT
        dst_x = outT if l == L - 1 else XT
        ps = psums()
        pr = Rot(ps[0:6])
        cs = load_consts(sb, [("one", C_ONE, C_ONE + 128, BF16, 128), ("onef", C_ONE, C_ONE + 128, F32, 128)])
        ones = cs["one"]
        onef = cs["onef"]
        stage = Rot([sb([128, 1024], F32) for _ in range(3)])
        sdsem = {id(s): kb.new_dsem() for s in stage.items}
        engs = Rot(["pool", "dve", "act"])
        Wo = sb([128, 8, D], BF16)
        load_w_bf16(Wo, lambda kc: Wo.t[:, kc, :], lambda kc: w_out[l, kc * 128:(kc + 1) * 128, :], 8, D, stage, sdsem, engs)
        Wpg = sb([128, 8, D], BF16)
        load_w_bf16(Wpg, lambda kc: Wpg.t[:, kc, :], lambda kc: ple_gate[l, kc * 128:(kc + 1) * 128, :], 8, D, stage, sdsem, engs)
        Wpp = sb([128, 2, D], BF16)
        load_w_bf16(Wpp, lambda kc: Wpp.t[:, kc, :], lambda kc: ple_proj[l, kc * 128:(kc + 1) * 128, :], 2, D, stage, sdsem, engs)
        colB = sb([128, 64], F32)
        kb.dma(colB.t[:], colsB[l], [], [colB.b], kb.new_dsem())
        eps_t = sb([128, 1], F32)
        kb.op("pool", lambda e: e.memset(eps_t.t[:], EPS), [], [eps_t.b])

        xf = [sb([128, 8, TT], F32) for _ in range(2)]
        mg = [sb([128, 8, TT], BF16) for _ in range(2)]
        pf = [sb([128, 2, TT], F32) for _ in range(2)]
        ld = [kb.new_dsem() for _ in range(2)]
        pb = sb([128, 2, TT], BF16)
        r = sb([128, 8, TT], F32)
        rb = sb([128, 8, TT], BF16)
        r2b = sb([128, 8, TT], BF16)
        sq = sb([128, 8, TT], BF16)
        xob = [sb([128, 8, TT], BF16) for _ in range(2)]
        gsb = Rot([sb([128, TT], F32) for _ in range(3)])
        tsb = Rot([sb([128, TT], F32) for _ in range(3)])
        mean = sb([128, TT], F32)
        msq = sb([128, TT], F32)
        rstd = sb([128, TT], F32)
        xo_d = [kb.new_dsem() for _ in range(2)]
        xob_d = [kb.new_dsem() for _ in range(2)]

        def issue_loads(t):
            i = t % 2
            sl = slice(t * TT, (t + 1) * TT)
            kb.dma(xf[i].t[:], src_x[:, sl].rearrange("(c p) t -> p c t", p=128), [], [xf[i].b], ld[i])
            kb.dma(mg[i].t[:], MGd[:, sl].rearrange("(c p) t -> p c t", p=128), [], [mg[i].b], ld[i])
            kb.dma(pf[i].t[:], pT_in[l, :, sl].rearrange("(c p) t -> p c t", p=128), [], [pf[i].b], ld[i])

        issue_loads(0)
        for t in range(NT):
            i = t % 2
            sl = slice(t * TT, (t + 1) * TT)
            if t + 1 < NT:
                issue_loads(t + 1)
            for kc in range(2):
                cp("pool", pb.t[:, kc, :], pf[i].t[:, kc, :], [pf[i].b], [pb.b])
            for dc in range(8):
                rp = pr.next()
                for kc in range(8):
                    mm(rp, rp.t[:], Wo.t[:, kc, dc * 128:(dc + 1) * 128], mg[i].t[:, kc, :], [Wo.b, mg[i].b], start=(kc == 0), stop=(kc == 7))
                stt("dve", r.t[:, dc, :], xf[i].t[:, dc, :], ALPHA, rp.t[:], ALU.mult, ALU.add, [xf[i].b, rp.b], [r.b])
                act(rb.t[:, dc, :], r.t[:, dc, :], AF.Copy, [r.b], [rb.b])
            s1 = ps[6]
            s2 = ps[7]
            for dc in range(8):
                up = pr.next()
                for kc in range(8):
                    mm(up, up.t[:], Wpg.t[:, kc, dc * 128:(dc + 1) * 128], rb.t[:, kc, :], [Wpg.b, rb.b], start=(kc == 0), stop=(kc == 7))
                g = gsb.next()
                act(g.t[:], up.t[:], AF.Tanh, [up.b], [g.b], scale=0.5)
                pq = pr.next()
                for kc in range(2):
                    mm(pq, pq.t[:], Wpp.t[:, kc, dc * 128:(dc + 1) * 128], pb.t[:, kc, :], [Wpp.b, pb.b], start=(kc == 0), stop=(kc == 1))
                tm = tsb.next()
                stt("dve", tm.t[:], g.t[:], 1.0, pq.t[:], ALU.add, ALU.mult, [pq.b, g.b], [tm.b])
                stt("dve", r.t[:, dc, :], tm.t[:], 0.5, r.t[:, dc, :], ALU.mult, ALU.add, [r.b, tm.b], [r.b])
                act(sq.t[:, dc, :], r.t[:, dc, :], AF.Square, [r.b], [sq.b])
                cp("dve", r2b.t[:, dc, :], r.t[:, dc, :], [r.b], [r2b.b])
            for dc in range(8):
                mm(s1, s1.t[:], ones.t[:, :], r2b.t[:, dc, :], [ones.b, r2b.b], start=(dc == 0), stop=(dc == 7))
            for dc in range(8):
                mm(s2, s2.t[:], ones.t[:, :], sq.t[:, dc, :], [ones.b, sq.b], start=(dc == 0), stop=(dc == 7))
            act(mean.t[:], s1.t[:], AF.Copy, [s1.b], [mean.b], scale=1.0 / D)
            tt("pool", msq.t[:], mean.t[:], mean.t[:], ALU.mult, [mean.b], [msq.b])
            stt("dve", msq.t[:], s2.t[:], 1.0 / D, msq.t[:], ALU.mult, ALU.subtract, [s2.b, msq.b], [msq.b])
            act(rstd.t[:], msq.t[:], AF.Ln, [msq.b, eps_t.b], [rstd.b], bias=eps_t.t[:, 0:1], scale=1.0)
            act(rstd.t[:], rstd.t[:], AF.Exp, [rstd.b], [rstd.b], scale=-0.5)
            o = xf[i]
            for dc in range(8):
                e1 = ("pool", "dve")[dc % 2]
                tt(e1, r.t[:, dc, :], r.t[:, dc, :], mean.t[:], ALU.subtract, [r.b, mean.b], [r.b])
                tt(e1, r.t[:, dc, :], r.t[:, dc, :], rstd.t[:], ALU.mult, [r.b, rstd.b], [r.b])
                act(o.t[:, dc, :], r.t[:, dc, :], AF.Identity, [r.b, colB.b], [o.b], scale=colB.t[:, dc:dc + 1], bias=colB.t[:, 8 + dc:9 + dc])
                if l < L - 1:
                    cp("pool", xob[i].t[:, dc, :], o.t[:, dc, :], [o.b], [xob[i].b])
            kb.dma(dst_x[:, sl].rearrange("(c p) t -> p c t", p=128), o.t[:], [o.b], [Buf()], xo_d[i])
            if l < L - 1:
                kb.dma(XTb[:, sl].rearrange("(c p) t -> p c t", p=128), xob[i].t[:], [xob[i].b], [Buf()], xob_d[i])

    _cm = nc.allow_non_contiguous_dma(reason="small strided scratch/const transfers")
    _cm.__enter__()
    phase(p0, "p0")
    cid = [0]

    def carry_tile(cst, shape, dt):
        cid[0] += 1
        return Tile(cst.enter_context(nc.sbuf_tensor("carry%d" % cid[0], list(shape), dt)))

    for l in range(L):
        phase(lambda sb, psums, l=l: p1a(l, sb, psums), "p1a")
        with ExitStack() as cst:
            pre = None
            if with_b and (only is None or ("p2" in only and "p1b" in only)):
                pre = {"Win": carry_tile(cst, [128, 8, 2056], BF16)}
            phase(lambda sb, psums, l=l: p2(l, sb, psums, pre), "p2")
            if with_b:
                phase(lambda sb, psums, l=l: p1b(l, sb, psums, pre), "p1b")
            else:
                phase(lambda sb, psums: pzero(1, sb, psums))
        cst2 = ExitStack()
        pre2 = None
        if with_c and (only is None or ("p1c" in only and "p3a" in only)):
            pre2 = {"Wg": carry_tile(cst2, [128, 8, 3072], BF16), "Wbr": carry_tile(cst2, [128, 12, D], BF16)}
        if with_c:
            phase(lambda sb, psums, l=l: p1c(l, sb, psums, pre2), "p1c")
        else:
            phase(lambda sb, psums: pzero(2, sb, psums))
        if dbg and l == L - 1:
            def pdbg(sb, psums):
                tl = sb([128, 12, T], BF16)
                d = kb.new_dsem()
                for n in range(3):
                    kb.dma(tl.t[:, 4 * n:4 * n + 4, :], Yd[n].rearrange("(c p) t -> p c t", p=128), [], [tl.b], d)
                d2 = kb.new_dsem()
                for n in range(3):
                    kb.dma(dbg_y[n].rearrange("(c p) t -> p c t", p=128), tl.t[:, 4 * n:4 * n + 4, :], [tl.b], [Buf()], d2)
            phase(pdbg)
        phase(lambda sb, psums, l=l: p3a(l, sb, psums, pre2), "p3a")
        cst2.close()
        phase(lambda sb, psums, l=l: p3b(l, sb, psums), "p3b")
    _cm.__exit__(None, None, None)
    return nc, kb


def prep_weights(inp, L):
    f = np.float32
    w_uq = np.asarray(inp["a_w_uq"], f)[:L].reshape(L, 256, 8, 96)
    w_uqn = np.ascontiguousarray(w_uq[:, :, :, 0:64].reshape(L, 256, 512))
    w_uqr = np.ascontiguousarray(w_uq[:, :, :, 64:96].reshape(L, 256, 256))
    w_ukv = np.asarray(inp["a_w_ukv"], f)[:L].reshape(L, 128, 8, 128)
    w_uk = np.ascontiguousarray(w_ukv[:, :, :, 0:64].reshape(L, 128, 512))
    w_uv = np.ascontiguousarray(w_ukv[:, :, :, 64:128].reshape(L, 128, 512))
    colsA = np.zeros((L, 128, 16), f)
    colsA[:, :, 0:2] = np.asarray(inp["a_q_norm"], f)[:L].reshape(L, 2, 128).transpose(0, 2, 1)
    colsA[:, :, 2] = np.asarray(inp["a_kv_norm"], f)[:L]
    colsB = np.zeros((L, 128, 64), f)
    colsB[:, :, 0:8] = np.asarray(inp["ln_g"], f)[:L].reshape(L, 8, 128).transpose(0, 2, 1)
    colsB[:, :, 8:16] = np.asarray(inp["ln_b"], f)[:L].reshape(L, 8, 128).transpose(0, 2, 1)
    colsC = np.zeros((L, 128, 32), f)
    colsC[:, :, 0] = np.asarray(inp["c_norm"], f)[:L]
    colsC[:, :, 1] = np.asarray(inp["b_norm"], f)[:L]
    lg = np.asarray(inp["c_lb_logits"], f).reshape(4, 4, 128)
    colsC[:, :, 8:24] = lg.transpose(2, 1, 0).reshape(128, 16)[None]
    colsW = np.ascontiguousarray(np.asarray(inp["b_conv"], f)[:L, :, 0, :].reshape(L, 4, 12, 128).transpose(0, 3, 2, 1))
    colsG = np.zeros((L, 128, 8), f)
    colsG[:, 0:4, 0] = np.asarray(inp["b_a_log"], f)[:L]
    colsG[:, 0:4, 1] = np.asarray(inp["b_dt_bias"], f)[:L]
    return dict(
        colsW=colsW, colsG=colsG, colsC=colsC, w_in=np.ascontiguousarray(np.asarray(inp["w_in"], f)[:L]),
        w_uqn=w_uqn, w_uqr=w_uqr, w_uk=w_uk, w_uv=w_uv, colsA=colsA, colsB=colsB,
        w_br=np.ascontiguousarray(np.asarray(inp["w_branch"], f)[:L]),
        w_out=np.ascontiguousarray(np.asarray(inp["w_out"], f)[:L]),
        ple_proj=np.ascontiguousarray(np.asarray(inp["ple_proj"], f)[:L]),
        ple_gate=np.ascontiguousarray(np.asarray(inp["ple_gate"], f)[:L]),
        cst=make_consts(),
    )


def run(inp, T, L, B, trace=False, **bkw):
    nc, kb = build(T, L, **bkw)
    shared = prep_weights(inp, L)
    x = np.asarray(inp["x"], np.float32)
    p = np.asarray(inp["p"], np.float32)
    pos = np.asarray(inp["positions"], np.int32)
    in_maps = []
    for b in range(B):
        m = dict(shared)
        m["xT"] = np.ascontiguousarray(x[b, :T].T)
        m["pT"] = np.ascontiguousarray(p[:L, b, :T].transpose(0, 2, 1))
        m["pos"] = np.ascontiguousarray(pos[b:b + 1, :T])
        in_maps.append(m)
    res = run_bass_kernel_spmd(nc, in_maps, core_ids=list(range(B)), **({"trace": True} if trace else {}))
    return res


def kernel(**inputs):
    res = run(inputs, 8192, 4, 4)
    out = np.stack([np.ascontiguousarray(r["outT"].T) for r in res.results], axis=0)
    return out.astype(np.float32)
```
